# Optimizing a Trainium2 kernel written in Bass

```python
import jax
import jax.numpy as jnp
from jax import lax
import numpy as np

D_MODEL = 1024
BATCH = 8
SEQ = 4096
DEPTH = 4

GRID_W = 64
CTX_LEN = 256
N_MIXERS = 3
CHUNK = 64
EPS = 1e-6
ALPHA = (2 * DEPTH) ** 0.25
BETA = (8 * DEPTH) ** -0.25

M_INNER = 2 * D_MODEL
M_HEADS = 4
M_DV = M_INNER // M_HEADS
M_DQK = M_DV // 2
M_CONV = 3

A_HEADS = 8
A_QLORA = 384
A_KVLORA = 256
A_DNOPE = 128
A_DROPE = 64
A_DV = 128
ROPE_BASE = 10000.0
Q_BLOCK = 128

G_HEADS = 4
G_DK = D_MODEL // (2 * G_HEADS)
G_DV = D_MODEL // G_HEADS
G_RANK = 16
G_TAU = 16.0

FFN_DIM = 2816
FFN_CONV = 3

kernel_name = "hybrid_mlstm_mla_gla_convffn_trunk"


def _count(kind):
    return len(range(kind, DEPTH, N_MIXERS))


def _layer_norm(x, g, b):
    xf = x.astype(jnp.float32)
    mu = jnp.mean(xf, -1, keepdims=True)
    var = jnp.mean(jnp.square(xf - mu), -1, keepdims=True)
    return ((xf - mu) * lax.rsqrt(var + EPS)).astype(x.dtype) * g + b


def _rms_norm(x, g):
    xf = x.astype(jnp.float32)
    return (xf * lax.rsqrt(jnp.mean(jnp.square(xf), -1, keepdims=True) + EPS)).astype(x.dtype) * g


def _head_rms_norm(x, g):
    B, T, H, d = x.shape
    return _rms_norm(x, g.reshape(H, d)).reshape(B, T, H * d)


def _dwconv(x, w, b):
    k = w.shape[0]
    y = lax.conv_general_dilated(x, w[:, None, :].astype(x.dtype), (1,), [(k // 2, k // 2)],
                                 dimension_numbers=("NWC", "WIO", "NWC"),
                                 feature_group_count=x.shape[-1])
    return y + b


def _modulation(cond, w, b):
    m = jax.nn.silu(cond) @ w + b
    return jnp.split(m[:, None, :], 6, axis=-1)


def _modulate(x, shift, scale):
    return x * (1 + scale) + shift


def _axial_angles(n_tokens):
    rows = n_tokens // GRID_W
    row = jnp.repeat(jnp.arange(rows, dtype=jnp.float32), GRID_W)
    col = jnp.tile(jnp.arange(GRID_W, dtype=jnp.float32), rows)
    n_freq = A_DROPE // 4
    inv_freq = ROPE_BASE ** (-jnp.arange(n_freq, dtype=jnp.float32) / n_freq)
    return row[:, None] * inv_freq, col[:, None] * inv_freq


def _rope_half(x, ang):
    x1, x2 = jnp.split(x, 2, axis=-1)
    cos = jnp.cos(ang).astype(x.dtype)
    sin = jnp.sin(ang).astype(x.dtype)
    return jnp.concatenate([x1 * cos - x2 * sin, x2 * cos + x1 * sin], -1)


def _axial_rope(x, ang_row, ang_col):
    xr, xc = jnp.split(x, 2, axis=-1)
    return jnp.concatenate([_rope_half(xr, ang_row), _rope_half(xc, ang_col)], -1)


def _time_flip(a, direction):
    return jnp.flip(a, axis=2) if direction == 1 else a


def _two_way(scan_fn, shared_c, gates_c, shared_l, gates_l, init):
    out_c, out_l = None, None
    for d in range(2):
        args_c = [_time_flip(a, d) for a in shared_c + tuple(g[d] for g in gates_c)]
        hc, state = scan_fn(*args_c, init)
        args_l = [_time_flip(a, d) for a in shared_l + tuple(g[d] for g in gates_l)]
        hl, _ = scan_fn(*args_l, state)
        hc, hl = _time_flip(hc, d), _time_flip(hl, d)
        out_c = hc if out_c is None else out_c + hc
        out_l = hl if out_l is None else out_l + hl
    return out_c, out_l


def _mlstm_chunk_scan(q, k, v, li, lf, state):
    B, H, T, _ = q.shape
    nc = T // CHUNK

    def chunks(a):
        a = a.astype(jnp.float32)
        return jnp.moveaxis(a.reshape(B, H, nc, CHUNK, *a.shape[3:]), 2, 0)

    tril = jnp.tril(jnp.ones((CHUNK, CHUNK), bool))

    def step(carry, inp):
        C, n, m = carry
        qc, kc, vc, ic, fc = inp
        b = jnp.cumsum(fc, axis=-1)
        dmat = jnp.where(tril, b[..., :, None] - b[..., None, :] + ic[..., None, :], -jnp.inf)
        inter = b + m[..., None]
        mj = jnp.maximum(inter, jnp.max(dmat, -1))
        wmat = jnp.exp(dmat - mj[..., None]) * jnp.einsum("bhjd,bhsd->bhjs", qc, kc)
        g = jnp.exp(inter - mj)
        num = g[..., None] * jnp.einsum("bhjd,bhde->bhje", qc, C) + jnp.einsum("bhjs,bhse->bhje", wmat, vc)
        den = g * jnp.einsum("bhjd,bhd->bhj", qc, n) + jnp.sum(wmat, -1)
        h = num / jnp.maximum(jnp.abs(den), jnp.exp(-mj))[..., None]
        bl = b[..., -1]
        ds = bl[..., None] - b + ic
        m_new = jnp.maximum(bl + m, jnp.max(ds, -1))
        kw = kc * jnp.exp(ds - m_new[..., None])[..., None]
        decay = jnp.exp(bl + m - m_new)
        C = decay[..., None, None] * C + jnp.einsum("bhsd,bhse->bhde", kw, vc)
        n = decay[..., None] * n + jnp.sum(kw, axis=2)
        return (C, n, m_new), h

    state, hs = lax.scan(step, state, tuple(chunks(a) for a in (q, k, v, li, lf)))
    return jnp.moveaxis(hs, 0, 2).reshape(B, H, T, -1), state


def _mlstm_inputs(h, w_up, conv_w, conv_b, w_qk, w_v, w_gates, b_gates):
    B, T, _ = h.shape
    x_m = h @ w_up
    x_c = jax.nn.silu(_dwconv(x_m, conv_w, conv_b))
    q, k = jnp.split(x_c @ w_qk, 2, axis=-1)
    q = q.reshape(B, T, M_HEADS, M_DQK).transpose(0, 2, 1, 3)
    k = k.reshape(B, T, M_HEADS, M_DQK).transpose(0, 2, 1, 3) * (M_DQK ** -0.5)
    v = (x_m @ w_v).reshape(B, T, M_HEADS, M_DV).transpose(0, 2, 1, 3)
    gates = jnp.einsum("bti,zig->zbgt", x_c, w_gates) + b_gates[:, None, :, None]
    li = gates[:, :, :M_HEADS].astype(jnp.float32)
    lf = jax.nn.log_sigmoid(gates[:, :, M_HEADS:].astype(jnp.float32))
    return (q, k, v), (li, lf)


def _mlstm_mixer(hl, hc, need_ctx, w_up, conv_w, conv_b, w_qk, w_v, w_gates, b_gates, w_og, norm_g, w_out):
    sh_l, gt_l = _mlstm_inputs(hl, w_up, conv_w, conv_b, w_qk, w_v, w_gates, b_gates)
    sh_c, gt_c = _mlstm_inputs(hc, w_up, conv_w, conv_b, w_qk, w_v, w_gates, b_gates)
    B = hl.shape[0]
    init = (jnp.zeros((B, M_HEADS, M_DQK, M_DV), jnp.float32),
            jnp.zeros((B, M_HEADS, M_DQK), jnp.float32),
            jnp.zeros((B, M_HEADS), jnp.float32))
    s_c, s_l = _two_way(_mlstm_chunk_scan, sh_c, gt_c, sh_l, gt_l, init)

    def out(h_in, s):
        Bq, T, _ = h_in.shape
        o = jax.nn.sigmoid(h_in @ w_og).reshape(Bq, T, M_HEADS, M_DV)
        hsum = o * s.transpose(0, 2, 1, 3).astype(h_in.dtype)
        return _head_rms_norm(hsum, norm_g) @ w_out

    return out(hl, s_l), (out(hc, s_c) if need_ctx else None)


def _mla_q(h, w_dq, q_norm, w_uq, ang):
    B, T, _ = h.shape
    q = (_rms_norm(h @ w_dq, q_norm) @ w_uq).reshape(B, T, A_HEADS, A_DNOPE + A_DROPE)
    if ang is None:
        return q
    q_nope, q_rope = jnp.split(q, [A_DNOPE], axis=-1)
    return jnp.concatenate([q_nope, _axial_rope(q_rope, ang[0][:, None], ang[1][:, None])], -1)


def _mla_kv(h, w_dkv, kv_norm, w_ukv, ang):
    B, T, _ = h.shape
    ckv, k_rope = jnp.split(h @ w_dkv, [A_KVLORA], axis=-1)
    if ang is not None:
        k_rope = _axial_rope(k_rope, ang[0], ang[1])
    kv = (_rms_norm(ckv, kv_norm) @ w_ukv).reshape(B, T, A_HEADS, A_DNOPE + A_DV)
    k_nope, v = jnp.split(kv, [A_DNOPE], axis=-1)
    k = jnp.concatenate([k_nope, jnp.broadcast_to(k_rope[:, :, None], (B, T, A_HEADS, A_DROPE))], -1)
    return k, v


def _attend(q, k, v):
    s = jnp.einsum("bqhd,bkhd->bhqk", q, k).astype(jnp.float32) * ((A_DNOPE + A_DROPE) ** -0.5)
    p = jax.nn.softmax(s, axis=-1).astype(v.dtype)
    return jnp.einsum("bhqk,bkhd->bqhd", p, v)


def _blocked_attend(q, k, v):
    B, T, H, d = q.shape
    qb = q.reshape(B, T // Q_BLOCK, Q_BLOCK, H, d).transpose(1, 0, 2, 3, 4)
    o = lax.map(lambda qi: _attend(qi, k, v), qb)
    return o.transpose(1, 0, 2, 3, 4).reshape(B, T, H, -1)


def _mla_mixer(hl, hc, need_ctx, ang, w_dq, q_norm, w_uq, w_dkv, kv_norm, w_ukv, w_out):
    B, T, _ = hl.shape
    kc, vc = _mla_kv(hc, w_dkv, kv_norm, w_ukv, None)
    kl, vl = _mla_kv(hl, w_dkv, kv_norm, w_ukv, ang)
    ql = _mla_q(hl, w_dq, q_norm, w_uq, ang)
    k_all = jnp.concatenate([kc, kl], axis=1)
    v_all = jnp.concatenate([vc, vl], axis=1)
    yl = _blocked_attend(ql, k_all, v_all).reshape(B, T, -1) @ w_out
    yc = None
    if need_ctx:
        qc = _mla_q(hc, w_dq, q_norm, w_uq, None)
        yc = _attend(qc, kc, vc).reshape(B, hc.shape[1], -1) @ w_out
    return yl, yc


def _gla_chunk_scan(q, k, v, la, S):
    B, H, T, _ = q.shape
    nc = T // CHUNK

    def chunks(a):
        a = a.astype(jnp.float32)
        return jnp.moveaxis(a.reshape(B, H, nc, CHUNK, *a.shape[3:]), 2, 0)

    tril = jnp.tril(jnp.ones((CHUNK, CHUNK), bool))

    def step(S, inp):
        qc, kc, vc, ac = inp
        b = jnp.cumsum(ac, axis=-2)
        inter = jnp.einsum("bhjd,bhde->bhje", qc * jnp.exp(b), S)
        rel = jnp.where(tril[:, :, None], b[:, :, :, None, :] - b[:, :, None, :, :], -jnp.inf)
        att = jnp.sum(qc[:, :, :, None, :] * kc[:, :, None, :, :] * jnp.exp(rel), axis=-1)
        o = inter + jnp.einsum("bhjs,bhse->bhje", att, vc)
        bl = b[:, :, -1:, :]
        S = jnp.exp(bl[:, :, 0, :, None]) * S + jnp.einsum("bhsd,bhse->bhde", kc * jnp.exp(bl - b), vc)
        return S, o

    S, os_ = lax.scan(step, S, tuple(chunks(a) for a in (q, k, v, la)))
    return jnp.moveaxis(os_, 0, 2).reshape(B, H, T, -1), S


def _gla_inputs(h, w_qk, w_v, w_a1, w_a2, b_a):
    B, T, _ = h.shape
    q, k = jnp.split(h @ w_qk, 2, axis=-1)
    q = q.reshape(B, T, G_HEADS, G_DK).transpose(0, 2, 1, 3) * (G_DK ** -0.5)
    k = k.reshape(B, T, G_HEADS, G_DK).transpose(0, 2, 1, 3)
    v = (h @ w_v).reshape(B, T, G_HEADS, G_DV).transpose(0, 2, 1, 3)
    z = jnp.einsum("zbtr,zrk->zbtk", jnp.einsum("btd,zdr->zbtr", h, w_a1), w_a2) + b_a[:, None, None, :]
    la = jax.nn.log_sigmoid(z.astype(jnp.float32)) / G_TAU
    la = la.reshape(2, B, T, G_HEADS, G_DK).transpose(0, 1, 3, 2, 4)
    return (q, k, v), (la,)


def _gla_mixer(hl, hc, need_ctx, w_qk, w_v, w_r, w_a1, w_a2, b_a, norm_g, w_out):
    sh_l, gt_l = _gla_inputs(hl, w_qk, w_v, w_a1, w_a2, b_a)
    sh_c, gt_c = _gla_inputs(hc, w_qk, w_v, w_a1, w_a2, b_a)
    init = jnp.zeros((hl.shape[0], G_HEADS, G_DK, G_DV), jnp.float32)
    s_c, s_l = _two_way(_gla_chunk_scan, sh_c, gt_c, sh_l, gt_l, init)

    def out(h_in, s):
        o = _head_rms_norm(s.transpose(0, 2, 1, 3).astype(h_in.dtype), norm_g)
        return (o * jax.nn.silu(h_in @ w_r)) @ w_out

    return out(hl, s_l), (out(hc, s_c) if need_ctx else None)


def _conv_ffn(h, w_in, conv_w, conv_b, w_out):
    gate, up = jnp.split(h @ w_in, 2, axis=-1)
    return (jax.nn.silu(_dwconv(gate, conv_w, conv_b)) * up) @ w_out


def setup_inputs(seed: int = 0) -> dict:
    key = jax.random.key(seed)
    keys = iter(jax.random.split(key, 48))
    D = D_MODEL
    nA, nB, nC = _count(0), _count(1), _count(2)

    def normal(shape):
        return jax.random.normal(next(keys), shape, jnp.float32)

    def dense(shape, fan_in, scale=1.0):
        return normal(shape) * (scale * fan_in ** -0.5)

    def gain(shape):
        return 1.0 + 0.05 * normal(shape)

    def bias(shape, s=0.02):
        return s * normal(shape)

    f_bias = jnp.asarray(np.linspace(3.0, 6.0, M_HEADS), jnp.float32)
    w_ukv = dense((nB, A_KVLORA, A_HEADS, A_DNOPE + A_DV), A_KVLORA)
    w_ukv = w_ukv.at[..., A_DNOPE:].multiply(BETA).reshape(nB, A_KVLORA, A_HEADS * (A_DNOPE + A_DV))
    return {
        "x": normal((BATCH, SEQ, D)),
        "c": normal((BATCH, D)),
        "ctx": normal((BATCH, CTX_LEN, D)),
        "c_ctx": normal((D,)),
        "ada_w": dense((DEPTH, D, 6 * D), D, 0.5),
        "ada_b": bias((DEPTH, 6 * D)),
        "ln_g": gain((DEPTH, 2, D)),
        "ln_b": bias((DEPTH, 2, D)),
        "ffn_w_in": dense((DEPTH, D, 2 * FFN_DIM), D, BETA),
        "ffn_conv_w": dense((DEPTH, FFN_CONV, FFN_DIM), FFN_CONV),
        "ffn_conv_b": bias((DEPTH, FFN_DIM)),
        "ffn_w_out": dense((DEPTH, FFN_DIM, D), FFN_DIM, BETA),
        "m_w_up": dense((nA, D, M_INNER), D),
        "m_conv_w": dense((nA, M_CONV, M_INNER), M_CONV),
        "m_conv_b": bias((nA, M_INNER)),
        "m_w_qk": dense((nA, M_INNER, 2 * M_HEADS * M_DQK), M_INNER),
        "m_w_v": dense((nA, M_INNER, M_HEADS * M_DV), M_INNER, BETA),
        "m_w_gates": dense((nA, 2, M_INNER, 2 * M_HEADS), M_INNER, 0.5),
        "m_b_gates": jnp.concatenate([0.1 * normal((nA, 2, M_HEADS)),
                                      f_bias + 0.1 * normal((nA, 2, M_HEADS))], axis=-1),
        "m_w_og": dense((nA, D, M_HEADS * M_DV), D),
        "m_norm_g": gain((nA, M_HEADS * M_DV)),
        "m_w_out": dense((nA, M_HEADS * M_DV, D), M_HEADS * M_DV, BETA),
        "a_w_dq": dense((nB, D, A_QLORA), D),
        "a_q_norm": gain((nB, A_QLORA)),
        "a_w_uq": dense((nB, A_QLORA, A_HEADS * (A_DNOPE + A_DROPE)), A_QLORA),
        "a_w_dkv": dense((nB, D, A_KVLORA + A_DROPE), D),
        "a_kv_norm": gain((nB, A_KVLORA)),
        "a_w_ukv": w_ukv,
        "a_w_out": dense((nB, A_HEADS * A_DV, D), A_HEADS * A_DV, BETA),
        "g_w_qk": dense((nC, D, 2 * G_HEADS * G_DK), D),
        "g_w_v": dense((nC, D, G_HEADS * G_DV), D, BETA),
        "g_w_r": dense((nC, D, G_HEADS * G_DV), D),
        "g_w_a1": dense((nC, 2, D, G_RANK), D),
        "g_w_a2": dense((nC, 2, G_RANK, G_HEADS * G_DK), G_RANK),
        "g_b_a": bias((nC, 2, G_HEADS * G_DK), 0.1),
        "g_norm_g": gain((nC, G_HEADS * G_DV)),
        "g_w_out": dense((nC, G_HEADS * G_DV, D), G_HEADS * G_DV, BETA),
    }


def reference(x, c, ctx, c_ctx, ada_w, ada_b, ln_g, ln_b,
              ffn_w_in, ffn_conv_w, ffn_conv_b, ffn_w_out,
              m_w_up, m_conv_w, m_conv_b, m_w_qk, m_w_v, m_w_gates, m_b_gates, m_w_og, m_norm_g, m_w_out,
              a_w_dq, a_q_norm, a_w_uq, a_w_dkv, a_kv_norm, a_w_ukv, a_w_out,
              g_w_qk, g_w_v, g_w_r, g_w_a1, g_w_a2, g_b_a, g_norm_g, g_w_out):
    ang = _axial_angles(x.shape[1])
    xc = ctx
    for i in range(DEPTH):
        need_ctx = i < DEPTH - 1
        kind, j = i % N_MIXERS, i // N_MIXERS
        ml = _modulation(c, ada_w[i], ada_b[i])
        mc = _modulation(c_ctx[None, :], ada_w[i], ada_b[i])
        hl = _modulate(x, ml[0], ml[1])
        hc = _modulate(xc, mc[0], mc[1])
        if kind == 0:
            yl, yc = _mlstm_mixer(hl, hc, need_ctx, m_w_up[j], m_conv_w[j], m_conv_b[j], m_w_qk[j], m_w_v[j],
                                  m_w_gates[j], m_b_gates[j], m_w_og[j], m_norm_g[j], m_w_out[j])
        elif kind == 1:
            yl, yc = _mla_mixer(hl, hc, need_ctx, ang, a_w_dq[j], a_q_norm[j], a_w_uq[j], a_w_dkv[j],
                                a_kv_norm[j], a_w_ukv[j], a_w_out[j])
        else:
            yl, yc = _gla_mixer(hl, hc, need_ctx, g_w_qk[j], g_w_v[j], g_w_r[j], g_w_a1[j], g_w_a2[j],
                                g_b_a[j], g_norm_g[j], g_w_out[j])
        x = _layer_norm(ALPHA * x + ml[2] * yl, ln_g[i, 0], ln_b[i, 0])
        f = _conv_ffn(_modulate(x, ml[3], ml[4]), ffn_w_in[i], ffn_conv_w[i], ffn_conv_b[i], ffn_w_out[i])
        x = _layer_norm(ALPHA * x + ml[5] * f, ln_g[i, 1], ln_b[i, 1])
        if need_ctx:
            xc = _layer_norm(ALPHA * xc + mc[2] * yc, ln_g[i, 0], ln_b[i, 0])
            fc = _conv_ffn(_modulate(xc, mc[3], mc[4]), ffn_w_in[i], ffn_conv_w[i], ffn_conv_b[i], ffn_w_out[i])
            xc = _layer_norm(ALPHA * xc + mc[5] * fc, ln_g[i, 1], ln_b[i, 1])
    return x
```

```python
import numpy as np
import concourse.bass as bass
import concourse.mybir as mybir

F32 = mybir.dt.float32
BF16 = mybir.dt.bfloat16
AF = mybir.ActivationFunctionType
ALU = mybir.AluOpType
AX = mybir.AxisListType

COMPUTE = ("pe", "dve", "act", "pool")
QUEUES = ("sp", "act", "pool")


class View:
    __slots__ = ("buf", "ap")

    def __init__(self, buf, ap):
        self.buf = buf
        self.ap = ap

    def __getitem__(self, idx):
        return View(self.buf, self.ap[idx])


class Buf:
    def __init__(self, key, t, space):
        self.key = key
        self.t = t
        self.space = space

    def __getitem__(self, idx):
        return View(self, self.t[idx])

    def k(self, sub):
        return Buf((self.key, sub), self.t, self.space)

    def full(self):
        return View(self, self.t.ap() if hasattr(self.t, "ap") else self.t[:])

    def view(self, pattern, **kw):
        a = self.t.ap() if hasattr(self.t, "ap") and not hasattr(self.t, "rearrange") else self.t
        return Buf(self.key, a.rearrange(pattern, **kw), self.space)

    def sub(self, idx):
        return Buf(self.key, self.t[idx], self.space)


class Phase:
    def __init__(self, P):
        self.P = P
        self.stack = None

    def __enter__(self):
        import contextlib
        self.stack = contextlib.ExitStack()
        self.stack.__enter__()
        return self

    def sbuf(self, name, shape, dtype):
        P = self.P
        P.nbuf += 1
        nm = f"{name}_{P.nbuf}"
        t = self.stack.enter_context(P.nc.sbuf_tensor(nm, list(shape), dtype))
        return Buf(nm, t, "sb")

    def __exit__(self, *a):
        if a[0] is None:
            self.P.barrier()
        return self.stack.__exit__(*a)


class Prog:
    def __init__(self, nc, nq=8, same_engine_sync=True):
        self.nc = nc
        self.same = same_engine_sync
        self.stream = {e: [] for e in ("pe", "dve", "act", "pool", "sp")}
        self.sem = {e: nc.alloc_semaphore("s_" + e) for e in COMPUTE}
        self.cnt = {e: 0 for e in COMPUTE}
        self.nq = nq
        self.qsem = {q: [nc.alloc_semaphore(f"q_{q}_{i}") for i in range(nq)] for q in QUEUES}
        self.qn = {q: 0 for q in QUEUES}
        self.qtok = {q: [None] * nq for q in QUEUES}
        self.known = {e: {} for e in self.stream}
        self.res = {}
        self.psacc = {}
        self.pskeys = set()
        self.nbuf = 0
        self.out_tokens = []
        self.n_inst = 0

    def sbuf(self, name, shape, dtype):
        t = self.nc.alloc_sbuf_tensor(name, list(shape), dtype)
        return Buf(name, t, "sb")

    def psum(self, name, shape, dtype=F32):
        t = self.nc.alloc_psum_tensor(name, list(shape), dtype)
        self.pskeys.add(name)
        return Buf(name, t, "ps")

    def dram(self, name, shape, dtype, kind="Internal"):
        t = self.nc.dram_tensor(name, list(shape), dtype, kind=kind)
        return Buf(name, t, "dr")

    def barrier(self):
        for eng in self.stream:
            for e in COMPUTE:
                if e != eng and self.cnt[e] > 0:
                    self._need(eng, (self.sem[e], self.cnt[e], e))
            for q in QUEUES:
                for tok in self.qtok[q]:
                    self._need(eng, tok)
        self.res = {}
        self.psacc = {}

    def phase(self):
        return Phase(self)

    def _need(self, eng, tok):
        if tok is None:
            return
        sem, val, owner = tok
        if owner == eng and owner in COMPUTE and (not self.same or eng == "pe"):
            return
        kn = self.known[eng]
        if kn.get(id(sem), 0) >= val:
            return
        kn[id(sem)] = val
        self.stream[eng].append(("wait", sem, val))

    def _deps(self, eng, reads, writes):
        for k in reads:
            r = self.res.get(k)
            if r is not None:
                self._need(eng, r["w"])
        for k in writes:
            r = self.res.get(k)
            if r is not None:
                self._need(eng, r["w"])
                for tok in r["r"].values():
                    if tok[2] == eng and eng in COMPUTE:
                        continue
                    self._need(eng, tok)

    def _commit(self, tok, reads, writes):
        for k in writes:
            self.res[k] = {"w": tok, "r": {}}
        for k in reads:
            if k in writes:
                continue
            r = self.res.setdefault(k, {"w": None, "r": {}})
            r["r"][id(tok[0])] = tok

    @staticmethod
    def _keys(views):
        ks = []
        for v in views:
            if v is None:
                continue
            if isinstance(v, View):
                k = v.buf.key
                sp = v.buf.space
            elif isinstance(v, Buf):
                k = v.key
                sp = v.space
            else:
                k = v
                sp = None
            if sp == "ps":
                while isinstance(k, tuple):
                    k = k[0]
            ks.append(k)
        return ks

    def op(self, eng, fns, reads=(), writes=()):
        reads = self._keys(reads)
        writes = self._keys(writes)
        self._deps(eng, reads, writes)
        psk = [k for k in reads + writes if k in self.pskeys]
        for k in psk:
            for e2, tok2 in self.psacc.get(k, {}).items():
                if e2 != eng:
                    self._need(eng, tok2)
        if not isinstance(fns, (list, tuple)):
            fns = [fns]
        self.cnt[eng] += 1
        tok = (self.sem[eng], self.cnt[eng], eng)
        for k in psk:
            self.psacc.setdefault(k, {})[eng] = tok
        for f in fns[:-1]:
            self.stream[eng].append(("op", f, None, 0))
        self.stream[eng].append(("op", fns[-1], self.sem[eng], 1))
        self._commit(tok, reads, writes)
        self.n_inst += len(fns)
        return tok

    def dma(self, q, out, in_, is_output=False, **kw):
        reads = self._keys([in_])
        writes = self._keys([out])
        n = self.qn[q]
        slot = n % self.nq
        self._need(q, self.qtok[q][slot])
        self._deps(q, reads, writes)
        self.qn[q] = n + 1
        sem = self.qsem[q][slot]
        tok = (sem, 16 * (n // self.nq + 1), "dma_" + q)
        self.qtok[q][slot] = tok
        o, i = out.ap, in_.ap
        self.stream[q].append(("op", lambda e: e.dma_start(out=o, in_=i, **kw), sem, 16))
        self._commit(tok, reads, writes)
        if is_output:
            self.out_tokens.append(tok)
        self.n_inst += 1
        return tok

    def dma_multi(self, q, out, in_, reads=None, writes=None, is_output=False, **kw):
        if reads is None:
            reads = [in_]
        if writes is None:
            writes = [out]
        reads = self._keys(reads)
        writes = self._keys(writes)
        n = self.qn[q]
        slot = n % self.nq
        self._need(q, self.qtok[q][slot])
        self._deps(q, reads, writes)
        self.qn[q] = n + 1
        sem = self.qsem[q][slot]
        tok = (sem, 16 * (n // self.nq + 1), "dma_" + q)
        self.qtok[q][slot] = tok
        o = out.ap if isinstance(out, View) else out
        i = in_.ap if isinstance(in_, View) else in_
        self.stream[q].append(("op", lambda e: e.dma_start(out=o, in_=i, **kw), sem, 16))
        self._commit(tok, reads, writes)
        if is_output:
            self.out_tokens.append(tok)
        self.n_inst += 1
        return tok

    def mm(self, out, pairs, extra_reads=()):
        n = len(pairs)
        fns = []
        rd = list(extra_reads)
        o = out.ap
        for j, (l, r) in enumerate(pairs):
            rd += [l, r]
            la, ra = l.ap, r.ap
            fns.append(lambda e, la=la, ra=ra, s=(j == 0), t=(j == n - 1): e.matmul(o, la, ra, start=s, stop=t))
        return self.op("pe", fns, reads=rd, writes=[out])

    def mm1(self, out, l, r, start, stop):
        o, la, ra = out.ap, l.ap, r.ap
        return self.op("pe", lambda e: e.matmul(o, la, ra, start=start, stop=stop), reads=[l, r], writes=[out])

    def mm_multi(self, groups, extra_reads=()):
        fns = []
        rd = list(extra_reads)
        wr = []
        for out, pairs in groups:
            n = len(pairs)
            o = out.ap
            wr.append(out)
            for j, (l, r) in enumerate(pairs):
                rd += [l, r]
                la, ra = l.ap, r.ap
                fns.append(lambda e, o=o, la=la, ra=ra, s=(j == 0), t=(j == n - 1): e.matmul(o, la, ra, start=s, stop=t))
        return self.op("pe", fns, reads=rd, writes=wr)

    def transpose(self, out, in_, ident):
        o, i, d = out.ap, in_.ap, ident.ap
        return self.op("pe", lambda e: e.transpose(o, i, d), reads=[in_, ident], writes=[out])

    def activation(self, eng, out, in_, func, bias=None, scale=None, accum_out=None):
        assert eng == "act"
        o, i = out.ap, in_.ap
        kw = {}
        rd = [in_]
        wr = [out]
        if bias is not None:
            if isinstance(bias, View):
                kw["bias"] = bias.ap
                rd.append(bias)
            else:
                kw["bias"] = bias
        if scale is not None:
            if isinstance(scale, View):
                kw["scale"] = scale.ap
                rd.append(scale)
            else:
                kw["scale"] = scale
        if accum_out is not None:
            kw["accum_out"] = accum_out.ap
            wr.append(accum_out)
        return self.op("act", lambda e: e.activation(o, i, func, **kw), reads=rd, writes=wr)

    def tt(self, eng, out, in0, in1, op):
        o, a, b = out.ap, in0.ap, in1.ap
        return self.op(eng, lambda e: e.tensor_tensor(o, a, b, op), reads=[in0, in1], writes=[out])

    def ts(self, eng, out, in0, s1, op0, s2=None, op1=None, accum_out=None):
        o, a = out.ap, in0.ap
        rd = [in0]
        wr = [out]
        if isinstance(s1, View):
            rd.append(s1)
            s1 = s1.ap
        if isinstance(s2, View):
            rd.append(s2)
            s2 = s2.ap
        kw = {}
        if op1 is not None:
            kw["op1"] = op1
        if accum_out is not None:
            kw["accum_out"] = accum_out.ap
            wr.append(accum_out)
        return self.op(eng, lambda e: e.tensor_scalar(o, a, s1, s2, op0, **kw), reads=rd, writes=wr)

    def stt(self, out, in0, scalar, in1, op0, op1, accum_out=None):
        o, a, b = out.ap, in0.ap, in1.ap
        rd = [in0, in1]
        wr = [out]
        if isinstance(scalar, View):
            rd.append(scalar)
            scalar = scalar.ap
        kw = {}
        if accum_out is not None:
            kw["accum_out"] = accum_out.ap
            wr.append(accum_out)
        return self.op("dve", lambda e: e.scalar_tensor_tensor(o, a, scalar, b, op0, op1, **kw), reads=rd, writes=wr)

    def copy(self, eng, out, in_):
        o, i = out.ap, in_.ap
        if eng == "act":
            return self.op(eng, lambda e: e.copy(o, i), reads=[in_], writes=[out])
        return self.op(eng, lambda e: e.tensor_copy(o, i), reads=[in_], writes=[out])

    def memset(self, eng, out, val):
        o = out.ap
        return self.op(eng, lambda e: e.memset(o, val), reads=[], writes=[out])

    def recip(self, out, in_):
        o, i = out.ap, in_.ap
        return self.op("dve", lambda e: e.reciprocal(o, i), reads=[in_], writes=[out])

    def emit(self):
        nc = self.nc
        for tok in self.out_tokens:
            self._need("sp", tok)
        for e in COMPUTE:
            if self.cnt[e] > 0:
                self._need("sp", (self.sem[e], self.cnt[e], e))
        for q in QUEUES:
            for tok in self.qtok[q]:
                self._need("sp", tok)
        streams = self.stream

        def run(eng, items):
            for it in items:
                if it[0] == "wait":
                    eng.wait_ge(it[1], it[2])
                else:
                    ins = it[1](eng)
                    if it[2] is not None:
                        ins.then_inc(it[2], it[3])

        with nc.Block() as block:
            @block.tensor
            def _(e):
                run(e, streams["pe"])

            @block.vector
            def _(e):
                run(e, streams["dve"])

            @block.scalar
            def _(e):
                run(e, streams["act"])

            @block.gpsimd
            def _(e):
                run(e, streams["pool"])

            @block.sync
            def _(e):
                run(e, streams["sp"])
from concourse.bass_utils import run_bass_kernel_spmd

D = 1024
KD = 8
T = 4096
TC = 256
TT = T + TC
DEPTH = 4
ALPHA = (2 * DEPTH) ** 0.25
BETA = (8 * DEPTH) ** -0.25
EPS = 1e-6
FFN = 2816
FK_FFN = 22
TILES512 = [(0, 256, 1)] + [(256 + 512 * i, 512, 0) for i in range(8)]
TILES256 = [(0, 256, 1)] + [(256 + 256 * i, 256, 0) for i in range(16)]
CH = 128
NCH = TT // CH


class G:
    pass


def load_w(P, ph, dst, src, r0, nrows, c0, ncols, stage, q="sp", engs=("act", "pool")):
    kcn = (nrows + 127) // 128
    cnt = 0
    for kc in range(kcn):
        pr = min(128, nrows - kc * 128)
        sw = stage[0].t.shape[1]
        for cs in range(0, ncols, sw):
            cw = min(sw, ncols - cs)
            st = stage[G.stage_i % len(stage)]
            G.stage_i += 1
            P.dma(q, st[0:pr, 0:cw], src[r0 + kc * 128: r0 + kc * 128 + pr, c0 + cs: c0 + cs + cw])
            e = engs[cnt % len(engs)]
            cnt += 1
            P.copy(e, dst.k(kc)[0:pr, kc, cs:cs + cw], st[0:pr, 0:cw])


def setup_globals(P, nc):
    g = G
    g.stage_i = 0
    g.ps = [P.psum(f"ps{i}", [128, 512], F32) for i in range(6)]
    g.psb = [P.psum(f"psb{i}", [128, 1024], BF16) for i in range(2)]
    g.MOD = P.sbuf("MOD", [128, DEPTH, 48, 2], F32)
    g.LNP = P.sbuf("LNP", [128, DEPTH * 2 * 2 * 8], F32)
    g.ones = P.sbuf("ones", [128, 128], BF16)
    g.ones32 = P.sbuf("ones32", [128, 128], F32)
    g.ident = P.sbuf("ident", [128, 128], BF16)
    g.tri_le = P.sbuf("tri_le", [128, 128], F32)
    g.tri_ge = P.sbuf("tri_ge", [128, 128], F32)
    P.memset("pool", g.ones[:, :], 1.0)
    P.memset("pool", g.ones32[:, :], 1.0)
    P.memset("pool", g.ident[:, :], 1.0)
    ia = g.ident.t[:, :]
    P.op("pool", lambda e: e.affine_select(ia, ia, [[-1, 128]], ALU.is_equal, 0.0, base=0, channel_multiplier=1),
         reads=[g.ident], writes=[g.ident])
    P.memset("pool", g.tri_le[:, :], 1.0)
    ta = g.tri_le.t[:, :]
    P.op("pool", lambda e: e.affine_select(ta, ta, [[1, 128]], ALU.is_ge, 0.0, base=0, channel_multiplier=-1),
         reads=[g.tri_le], writes=[g.tri_le])
    P.memset("pool", g.tri_ge[:, :], 1.0)
    tb = g.tri_ge.t[:, :]
    P.op("pool", lambda e: e.affine_select(tb, tb, [[-1, 128]], ALU.is_ge, 0.0, base=0, channel_multiplier=1),
         reads=[g.tri_ge], writes=[g.tri_ge])
    P.dma("sp", g.LNP[:, :], g.inp["ln_fm"][:, :])


def phase_mod(P):
    g = G
    with P.phase() as ph:
        cv = ph.sbuf("cv", [128, 8, 2], F32)
        sc = ph.sbuf("sc", [128, 8, 2], F32)
        ab = ph.sbuf("ab", [128, DEPTH * 48], F32)
        wst = [ph.sbuf(f"wst{i}", [128, 8, 512], F32) for i in range(3)]
        P.dma("sp", cv[:, :, :], g.inp["cv"][:, :, :])
        P.dma("sp", ab[:, :], g.inp["ada_b_fm"][:, :])
        P.activation("act", sc[:, :, :], cv[:, :, :], AF.Silu)
        aw = g.inp["ada_w"].view("l (k p) n -> l p k n", p=128)
        psm = g.ps[0]
        for l in range(DEPTH):
            for piece in range(12):
                w = wst[(l * 12 + piece) % 3]
                P.dma("sp" if piece % 2 == 0 else "act", w[:, :, :], aw[l, :, :, piece * 512:(piece + 1) * 512])
                for jj in range(4):
                    j = piece * 4 + jj
                    P.mm(psm[:, 2 * j:2 * j + 2],
                         [(w[:, kc, jj * 128:(jj + 1) * 128], sc[:, kc, :]) for kc in range(8)])
            pv = psm.view("p (j c) -> p j c", c=2)
            for cnd in range(2):
                P.tt("dve", g.MOD[:, l, :, cnd], pv[:, 0:48, cnd], ab[:, l * 48:(l + 1) * 48], ALU.add)


def mod_vec(l, which, cond):
    return G.MOD[:, l, which * 8:(which + 1) * 8, cond]


def ln_vec(l, which, gb):
    o = ((l * 2 + which) * 2 + gb) * 8
    return G.LNP[:, o:o + 8]


def phase_modulate_first(P, l, sh_i, sc_i):
    g = G
    with P.phase() as ph:
        s1 = ph.sbuf("s1", [128, 2, 8], F32)
        for cnd in range(2):
            P.ts("dve", s1[:, cnd, :], mod_vec(l, sc_i, cnd), 1.0, ALU.add)
        xs = [ph.sbuf(f"x{i}", [128, 8, 512], F32) for i in range(2)]
        hs = [ph.sbuf(f"h{i}", [128, 8, 512], BF16) for i in range(2)]
        xv = g.XR.view("(k p) n -> p k n", p=128)
        hv = g.HT.view("(k p) n -> p k n", p=128)
        for ti, (s, n, cnd) in enumerate(TILES512):
            x = xs[ti % 2]
            h = hs[ti % 2]
            P.dma("sp", x[:, :, 0:n], xv.k(ti)[:, :, s:s + n])
            sh = mod_vec(l, sh_i, cnd)
            for dc in range(8):
                P.activation("act", h[:, dc, 0:n], x[:, dc, 0:n], AF.Identity,
                             scale=s1[:, cnd, dc:dc + 1], bias=sh[:, dc:dc + 1])
            P.dma("pool", hv.k(ti)[:, :, s:s + n], h[:, :, 0:n])


def phase_ffn1(P, l):
    g = G
    with P.phase() as ph:
        hT = ph.sbuf("hT", [128, 8, TT], BF16)
        hv = g.HT.view("(k p) n -> p k n", p=128)
        for kc in range(8):
            P.dma("sp" if kc % 2 == 0 else "act", hT.k(kc)[:, kc, :], hv[:, kc, :])
        cw = ph.sbuf("cw", [128, FK_FFN, 3], F32)
        cb = ph.sbuf("cb", [128, FK_FFN], F32)
        P.dma("sp", cw[:, :, :], g.inp["ffn_cw_fm"][l, :, :, :])
        P.dma("sp", cb[:, :], g.inp["ffn_cb_fm"][l, :, :])
        GW = TT + 4
        Gb = [ph.sbuf(f"G{i}", [128, GW], F32) for i in range(2)]
        Tb = [ph.sbuf(f"Tm{i}", [128, TT], F32) for i in range(2)]
        Ab = [ph.sbuf(f"Ar{i}", [128, TT], BF16) for i in range(2)]
        wst = [ph.sbuf(f"wst{i}", [128, 8, 128], F32) for i in range(4)]
        wb = [ph.sbuf(f"wb{i}", [128, 8, 128], BF16) for i in range(4)]
        for i in range(2):
            P.op("pool", lambda e, i=i: e.memset(Gb[i].t[:, :], 0.0), reads=[], writes=[Gb[i].k(ti) for ti in range(9)])
        win = g.inp["ffn_w_in"].view("l (k p) n -> l p k n", p=128)
        atv = g.AT.view("(f p) n -> f p n", p=128)
        def goff(cnd):
            return 1 if cnd == 1 else 3

        def load_pair(f):
            res = []
            for half in range(2):
                i = (f % 2) * 2 + half
                c0 = half * FFN + f * 128
                P.dma("sp" if half == 0 else "act", wst[i][:, :, :], win[l, :, :, c0:c0 + 128])
                P.copy("pool", wb[i][:, :, :], wst[i][:, :, :])
                res.append(wb[i])
            return res

        def gate_mm(f, wg):
            Gf = Gb[f % 2]
            for ti, (s, n, cnd) in enumerate(TILES512):
                ps = g.ps[ti % 3]
                P.mm(ps[:, 0:n], [(wg[:, kc, :], hT.k(kc)[:, kc, s:s + n]) for kc in range(8)])
                o = goff(cnd) + s
                P.activation("act", Gf.k(ti)[:, o:o + n], ps[:, 0:n], AF.Copy)

        def conv(f):
            Gf = Gb[f % 2]
            Tf = Tb[f % 2]
            gr = [Gf.k(ti) for ti in range(9)]
            for (s, n, o) in ((0, TC, 1), (TC, T, 3)):
                P.op("dve", lambda e, o=o, s=s, n=n: e.tensor_scalar(Tf.t[:, s:s + n], Gf.t[:, o + s - 1:o + s - 1 + n],
                                                                     cw.t[:, f, 0:1], None, ALU.mult),
                     reads=gr + [cw], writes=[Tf])
                for tap in (1, 2):
                    P.op("dve", lambda e, o=o, s=s, n=n, tap=tap: e.scalar_tensor_tensor(
                        Tf.t[:, s:s + n], Gf.t[:, o + s - 1 + tap:o + s - 1 + tap + n], cw.t[:, f, tap:tap + 1],
                        Tf.t[:, s:s + n], ALU.mult, ALU.add), reads=gr + [cw, Tf], writes=[Tf])
                P.op("act", lambda e, s=s, n=n: e.activation(Tf.t[:, s:s + n], Tf.t[:, s:s + n], AF.Silu,
                                                             bias=cb.t[:, f:f + 1]), reads=[Tf, cb], writes=[Tf])

        def up_mm(f, wu):
            Tf = Tb[f % 2]
            Af = Ab[f % 2]
            for ti, (s, n, cnd) in enumerate(TILES512):
                ps = g.ps[3 + ti % 3]
                P.mm(ps[:, 0:n], [(wu[:, kc, :], hT.k(kc)[:, kc, s:s + n]) for kc in range(8)])
                P.tt("dve", Af.k(ti)[:, s:s + n], ps[:, 0:n], Tf[:, s:s + n], ALU.mult)
            P.dma_multi("pool", atv.k(f)[f, :, :], Af.t[:, :], reads=[Af.k(ti) for ti in range(9)])

        prev = None
        for f in range(FK_FFN):
            wg, wu = load_pair(f)
            gate_mm(f, wg)
            if prev is not None:
                up_mm(prev[0], prev[1])
            conv(f)
            prev = (f, wu)
        up_mm(prev[0], prev[1])


def phase_out_ln(P, l, which, FK, wsrc, gate_idx, next_mod, final=False, skip_ctx=False):
    g = G
    with P.phase() as ph:
        W = ph.sbuf("W", [128, FK, D], BF16)
        stage = [ph.sbuf(f"st{i}", [128, 1024], F32) for i in range(2)]
        load_w(P, ph, W, wsrc, 0, FK * 128, 0, D, stage)
        ga = ph.sbuf("ga", [128, 2, 8], F32)
        Av = ph.sbuf("Av", [128, 2, 8], F32)
        Bv = ph.sbuf("Bv", [128, 2, 8], F32)
        lng = ln_vec(l, which, 0)
        lnb = ln_vec(l, which, 1)
        for cnd in range(2):
            P.ts("dve", ga[:, cnd, :], mod_vec(l, gate_idx, cnd), 1.0 / ALPHA, ALU.mult)
            if next_mod is not None:
                nl, sh_i, sc_i = next_mod
                P.ts("dve", Av[:, cnd, :], mod_vec(nl, sc_i, cnd), 1.0, ALU.add)
                P.tt("dve", Bv[:, cnd, :], Av[:, cnd, :], lnb, ALU.mult)
                P.tt("dve", Bv[:, cnd, :], Bv[:, cnd, :], mod_vec(nl, sh_i, cnd), ALU.add)
                P.tt("dve", Av[:, cnd, :], Av[:, cnd, :], lng, ALU.mult)
        NB = 2
        ab = [ph.sbuf(f"a{i}", [128, FK, 256], BF16) for i in range(NB)]
        xb = [ph.sbuf(f"x{i}", [128, 8, 256], F32) for i in range(NB)]
        ub = [ph.sbuf(f"u{i}", [128, 8, 256], F32) for i in range(NB)]
        u16 = [ph.sbuf(f"ub{i}", [128, 8, 256], BF16) for i in range(NB)]
        sq16 = [ph.sbuf(f"sq{i}", [128, 8, 256], BF16) for i in range(NB)]
        hb = [ph.sbuf(f"h{i}", [128, 8, 256], BF16) for i in range(NB)]
        st = [[ph.sbuf(f"s{i}_{j}", [128, 256], F32) for j in range(5)] for i in range(NB)]
        atv = g.AT.view("(f p) n -> p f n", p=128)
        xv = g.XR.view("(k p) n -> p k n", p=128)
        hv = g.HT.view("(k p) n -> p k n", p=128)
        ov = g.OUT.view("(k p) n -> p k n", p=128) if final else None
        eps2 = EPS / (ALPHA * ALPHA)
        for ti, (s, n, cnd) in enumerate(TILES256):
            if skip_ctx and cnd == 1:
                continue
            b = ti % NB
            a, x, u, u6, s6, h = ab[b], xb[b], ub[b], u16[b], sq16[b], hb[b]
            mean, m2, var, sd, rstd = st[b]
            P.dma("sp", a[:, :, 0:n], atv.k(ti)[:, 0:FK, s:s + n])
            P.dma_multi("act", x.t[:, :, 0:n], xv.k(ti)[:, :, s:s + n], writes=[x.k(dc) for dc in range(8)])
            for dc in range(8):
                ps = g.ps[dc % 4]
                P.mm(ps[:, 0:n], [(W.k(fk)[:, fk, dc * 128:(dc + 1) * 128], a[:, fk, 0:n]) for fk in range(FK)])
                P.stt(u.k(dc)[:, dc, 0:n], ps[:, 0:n], ga[:, cnd, dc:dc + 1], x.k(dc)[:, dc, 0:n], ALU.mult, ALU.add)
                P.copy("act", u6.k(dc)[:, dc, 0:n], u.k(dc)[:, dc, 0:n])
                P.tt("pool", s6.k(dc)[:, dc, 0:n], u.k(dc)[:, dc, 0:n], u.k(dc)[:, dc, 0:n], ALU.mult)
            S1 = g.ps[4]
            S2 = g.ps[5]
            P.mm(S1[:, 0:n], [(g.ones[:, :], u6.k(dc)[:, dc, 0:n]) for dc in range(8)])
            P.mm(S2[:, 0:n], [(g.ones[:, :], s6.k(dc)[:, dc, 0:n]) for dc in range(8)])
            P.activation("act", mean[:, 0:n], S1[:, 0:n], AF.Copy, scale=1.0 / D)
            P.tt("dve", m2[:, 0:n], mean[:, 0:n], mean[:, 0:n], ALU.mult)
            P.stt(var[:, 0:n], S2[:, 0:n], 1.0 / D, m2[:, 0:n], ALU.mult, ALU.subtract)
            P.ts("dve", var[:, 0:n], var[:, 0:n], eps2, ALU.add)
            P.activation("act", sd[:, 0:n], var[:, 0:n], AF.Sqrt)
            P.recip(rstd[:, 0:n], sd[:, 0:n])
            P.tt("dve", m2[:, 0:n], mean[:, 0:n], rstd[:, 0:n], ALU.mult)
            for dc in range(8):
                P.tt("dve", u.k(dc)[:, dc, 0:n], u.k(dc)[:, dc, 0:n], rstd[:, 0:n], ALU.mult)
                P.tt("pool", u.k(dc)[:, dc, 0:n], u.k(dc)[:, dc, 0:n], m2[:, 0:n], ALU.subtract)
                P.activation("act", x.k(dc)[:, dc, 0:n], u.k(dc)[:, dc, 0:n], AF.Identity,
                             scale=lng[:, dc:dc + 1], bias=lnb[:, dc:dc + 1])
                if next_mod is not None:
                    P.activation("act", h.k(dc)[:, dc, 0:n], u.k(dc)[:, dc, 0:n], AF.Identity,
                                 scale=Av[:, cnd, dc:dc + 1], bias=Bv[:, cnd, dc:dc + 1])
            xr = [x.k(dc) for dc in range(8)]
            if final:
                if cnd == 0:
                    P.dma_multi("pool", ov[:, :, s - TC:s - TC + n], x.t[:, :, 0:n], reads=xr, is_output=True)
            else:
                P.dma_multi("pool", xv.k(ti)[:, :, s:s + n], x.t[:, :, 0:n], reads=xr)
            if next_mod is not None:
                P.dma_multi("pool", hv.k(ti)[:, :, s:s + n], h.t[:, :, 0:n], reads=[h.k(dc) for dc in range(8)])

G_H = 4
G_DK = 128
G_DV = 256


def scan_order(d):
    nctx = TC // CH
    if d == 0:
        return list(range(0, nctx)) + list(range(nctx, NCH))
    return list(range(nctx - 1, -1, -1)) + list(range(NCH - 1, nctx - 1, -1))


def phase_gla_proj(P):
    g = G
    g.gQ = P.dram("gQ", [TT, 512], BF16)
    g.gK = P.dram("gK", [TT, 512], BF16)
    g.gV = P.dram("gV", [TT, 1024], BF16)
    g.gR = P.dram("gR", [TT, 1024], BF16)
    g.gL = P.dram("gL", [2, TT, 512], F32)
    g.gO = P.dram("gO", [TT, 1024], F32)
    with P.phase() as ph:
        hT = ph.sbuf("hT", [128, 8, TT], BF16)
        hv = g.HT.view("(k p) n -> p k n", p=128)
        for kc in range(8):
            P.dma("sp" if kc % 2 == 0 else "act", hT.k(kc)[:, kc, :], hv[:, kc, :])
        stage = [ph.sbuf(f"st{i}", [128, 1024], F32) for i in range(2)]
        Wqk = ph.sbuf("Wqk", [128, 8, 1024], BF16)
        Wv = ph.sbuf("Wv", [128, 8, 1024], BF16)
        Wr = ph.sbuf("Wr", [128, 8, 1024], BF16)
        Wa1 = ph.sbuf("Wa1", [128, 8, 32], BF16)
        Wa2 = ph.sbuf("Wa2", [17, 2, 512], F32)
        load_w(P, ph, Wqk, g.inp["g_w_qk"].sub(0), 0, 1024, 0, 1024, stage)
        load_w(P, ph, Wv, g.inp["g_w_v"].sub(0), 0, 1024, 0, 1024, stage)
        load_w(P, ph, Wr, g.inp["g_w_r"].sub(0), 0, 1024, 0, 1024, stage)
        load_w(P, ph, Wa1, g.inp["g_wa1"], 0, 1024, 0, 32, stage)
        P.dma("sp", Wa2[:, :, :], g.inp["g_wa2b"].view("d r n -> r d n")[:, :, :])
        r1 = [ph.sbuf(f"r1_{d}", [17, 128], F32) for d in range(2)]
        for d in range(2):
            P.memset("pool", r1[d][:, :], 1.0)
        NB = 2
        qk_t = [ph.sbuf(f"qk{i}", [128, 1024], BF16) for i in range(NB)]
        v_t = [ph.sbuf(f"v{i}", [128, 1024], BF16) for i in range(NB)]
        r_t = [ph.sbuf(f"r{i}", [128, 1024], BF16) for i in range(NB)]
        e_t = [ph.sbuf(f"e{i}", [128, 2, 512], F32) for i in range(NB)]
        l_t = [ph.sbuf(f"l{i}", [128, 2, 512], F32) for i in range(NB)]
        for c in range(NCH):
            b = c % NB
            tok = slice(c * CH, (c + 1) * CH)
            lhs = [hT.k(kc)[:, kc, tok] for kc in range(8)]
            for half in range(2):
                ps = g.ps[half]
                P.mm(ps[:, :], [(lhs[kc], Wqk.k(kc)[:, kc, half * 512:(half + 1) * 512]) for kc in range(8)])
                P.copy("dve", qk_t[b].k(half)[:, half * 512:(half + 1) * 512], ps[:, :])
            P.dma("pool", g.gQ.k(c)[tok, :], qk_t[b].k(0)[:, 0:512])
            P.dma("pool", g.gK.k(c)[tok, :], qk_t[b].k(1)[:, 512:1024])
            for half in range(2):
                ps = g.ps[2 + half]
                P.mm(ps[:, :], [(lhs[kc], Wv.k(kc)[:, kc, half * 512:(half + 1) * 512]) for kc in range(8)])
                P.copy("pool" if False else "dve", v_t[b].k(half)[:, half * 512:(half + 1) * 512], ps[:, :])
            P.dma_multi("pool", g.gV.k(c)[tok, :], v_t[b].t[:, :], reads=[v_t[b].k(0), v_t[b].k(1)])
            for half in range(2):
                ps = g.ps[4 + half]
                P.mm(ps[:, :], [(lhs[kc], Wr.k(kc)[:, kc, half * 512:(half + 1) * 512]) for kc in range(8)])
                P.activation("act", r_t[b].k(half)[:, half * 512:(half + 1) * 512], ps[:, :], AF.Silu)
            P.dma_multi("pool", g.gR.k(c)[tok, :], r_t[b].t[:, :], reads=[r_t[b].k(0), r_t[b].k(1)])
            for d in range(2):
                ps = g.ps[2 + d]
                P.mm(ps[0:16, d * 128:(d + 1) * 128],
                     [(Wa1.k(kc)[:, kc, d * 16:(d + 1) * 16], hT.k(kc)[:, kc, tok]) for kc in range(8)])
                P.copy("dve", r1[d][0:16, :], ps[0:16, d * 128:(d + 1) * 128])
            for d in range(2):
                ps = g.ps[d]
                P.mm(ps[:, :], [(r1[d][0:17, :], Wa2[0:17, d, :])])
                P.activation("act", e_t[b].k(d)[:, d, :], ps[:, :], AF.Exp, scale=-1.0)
            for d in range(2):
                P.activation("act", l_t[b].k(d)[:, d, :], e_t[b].k(d)[:, d, :], AF.Ln, bias=1.0)
                P.dma("pool", g.gL.k((d, c))[d, tok, :], l_t[b].k(d)[:, d, :])


def phase_gla_scan(P):
    g = G
    with P.phase() as ph:
        triN = []
        for nm, src in (("tnl", g.tri_le), ("tng", g.tri_ge)):
            t = ph.sbuf(nm, [128, 128], F32)
            P.ts("dve", t[:, :], src[:, :], -1.0 / 16.0, ALU.mult)
            triN.append(t)
        negc = ph.sbuf("negc", [128, 2], F32)
        P.memset("pool", negc[:, :], -1.0 / 16.0)
        gbc = ph.sbuf("gbc", [128, 1024], F32)
        P.dma("sp", gbc[:, :], g.inp["g_ng_bc"][0, :, :])
        S = [ph.sbuf(f"S{h}", [128, G_DV], F32) for h in range(G_H)]
        Sb = [ph.sbuf(f"Sb{h}", [128, G_DV], BF16) for h in range(G_H)]
        NB = 2
        q_t = [ph.sbuf(f"q{i}", [128, 512], BF16) for i in range(NB)]
        k_t = [ph.sbuf(f"k{i}", [128, 512], BF16) for i in range(NB)]
        v_t = [ph.sbuf(f"v{i}", [128, 1024], BF16) for i in range(NB)]
        L_t = [ph.sbuf(f"L{i}", [128, 512], F32) for i in range(NB)]
        o_t = [ph.sbuf(f"o{i}", [128, 1024], F32) for i in range(NB)]
        o0_t = [ph.sbuf(f"oo{i}", [128, 1024], F32) for i in range(NB)]
        r_t = [ph.sbuf(f"r{i}", [128, 1024], BF16) for i in range(NB)]
        a_t = [ph.sbuf(f"a{i}", [128, 1024], BF16) for i in range(NB)]
        aT = [ph.sbuf(f"aT{i}", [128, 8, 128], BF16) for i in range(NB)]
        tmp = [ph.sbuf(f"tmp{i}", [128, 1024], F32) for i in range(NB)]
        ss = [ph.sbuf(f"ss{i}", [128, 4], F32) for i in range(NB)]
        junk = ph.sbuf("junk", [128, 256], F32)
        HB = 2
        Ep = [ph.sbuf(f"Ep{i}", [128, 128], F32) for i in range(HB)]
        En = [ph.sbuf(f"En{i}", [128, 128], F32) for i in range(HB)]
        ebl = [ph.sbuf(f"ebl{i}", [128, 2], F32) for i in range(HB)]
        qs = [ph.sbuf(f"qs{i}", [128, 128], BF16) for i in range(HB)]
        ks = [ph.sbuf(f"ks{i}", [128, 128], BF16) for i in range(HB)]
        qsT = [ph.sbuf(f"qsT{i}", [128, 128], BF16) for i in range(HB)]
        ksT = [ph.sbuf(f"ksT{i}", [128, 128], BF16) for i in range(HB)]
        attT = [ph.sbuf(f"att{i}", [128, 128], BF16) for i in range(HB)]
        atv = g.AT.view("(f p) n -> p f n", p=128)
        it = 0
        for d in range(2):
            for h in range(G_H):
                P.memset("pool", S[h][:, :], 0.0)
                P.memset("pool", Sb[h][:, :], 0.0)
            mask = g.tri_le if d == 0 else g.tri_ge
            for ci, c in enumerate(scan_order(d)):
                b = ci % NB
                tok = slice(c * CH, (c + 1) * CH)
                P.dma("sp", q_t[b][:, :], g.gQ[tok, :])
                P.dma("act", k_t[b][:, :], g.gK[tok, :])
                P.dma("sp", v_t[b][:, :], g.gV[tok, :])
                P.dma("act", L_t[b][:, :], g.gL[d, tok, :])
                if d == 1:
                    P.dma("sp", o0_t[b][:, :], g.gO.k(c)[tok, :])
                    P.dma("act", r_t[b][:, :], g.gR[tok, :])
                for h in range(G_H):
                    hb = it % HB
                    it += 1
                    hs = slice(h * 128, (h + 1) * 128)
                    vs = slice(h * G_DV, (h + 1) * G_DV)
                    bps = g.ps[hb]
                    P.mm(bps[:, 0:128], [(triN[d][:, :], L_t[b][:, hs])])
                    blp = g.ps[hb]
                    P.mm(blp[:, 128 + 2 * hb:128 + 2 * hb + 2], [(L_t[b][:, hs], negc[:, :])])
                    P.activation("act", Ep[hb][:, :], bps[:, 0:128], AF.Exp)
                    P.activation("act", En[hb][:, :], bps[:, 0:128], AF.Exp, scale=-1.0)
                    P.activation("act", ebl[hb][:, :], blp[:, 128 + 2 * hb:128 + 2 * hb + 2], AF.Exp)
                    P.stt(qs[hb][:, :], q_t[b][:, hs], float(G_DK ** -0.5), Ep[hb][:, :], ALU.mult, ALU.mult)
                    P.tt("pool", ks[hb][:, :], k_t[b][:, hs], En[hb][:, :], ALU.mult)
                    tq = g.psb[hb][:, 0:128]
                    tk = g.psb[hb][:, 128:256]
                    P.transpose(tq, qs[hb][:, :], g.ident[:, :])
                    P.transpose(tk, ks[hb][:, :], g.ident[:, :])
                    P.copy("dve", qsT[hb][:, :], tq)
                    P.copy("act", ksT[hb][:, :], tk)
                    sps = g.ps[2 + hb]
                    P.mm(sps[:, 0:128], [(ksT[hb][:, :], qsT[hb][:, :])])
                    P.tt("dve", attT[hb][:, :], sps[:, 0:128], mask[:, :], ALU.mult)
                    ops = g.ps[4 + hb]
                    P.mm(ops[:, 0:G_DV], [(attT[hb][:, :], v_t[b][:, vs]), (qsT[hb][:, :], Sb[h][:, :])])
                    if d == 0:
                        P.copy("act", o_t[b].k(h)[:, vs], ops[:, 0:G_DV])
                    else:
                        P.tt("dve", o_t[b].k(h)[:, vs], ops[:, 0:G_DV], o0_t[b][:, vs], ALU.add)
                    dps = g.ps[2 + hb]
                    P.mm(dps[:, 128:384], [(ks[hb][:, :], v_t[b][:, vs])])
                    P.tt("dve", S[h][:, :], dps[:, 128:384], S[h][:, :], ALU.add)
                    P.activation("act", S[h][:, :], S[h][:, :], AF.Identity, scale=ebl[hb][:, 0:1])
                    P.copy("pool", Sb[h][:, :], S[h][:, :])
                okeys = [o_t[b].k(h) for h in range(G_H)]
                if d == 0:
                    P.dma_multi("pool", g.gO.k(c)[tok, :], o_t[b].t[:, :], reads=okeys)
                else:
                    for h in range(G_H):
                        vs = slice(h * G_DV, (h + 1) * G_DV)
                        P.op("act", lambda e, b=b, h=h, vs=vs: e.activation(junk.t[:, :], o_t[b].t[:, vs], AF.Square,
                                                                          accum_out=ss[b].t[:, h:h + 1]),
                             reads=[o_t[b].k(h)], writes=[junk, ss[b].k(h)])
                    sk = [ss[b].k(h) for h in range(G_H)]
                    P.op("dve", lambda e, b=b: e.tensor_scalar(ss[b].t[:, :], ss[b].t[:, :], 1.0 / G_DV, EPS, ALU.mult, ALU.add),
                         reads=sk, writes=sk)
                    P.op("act", lambda e, b=b: e.activation(ss[b].t[:, :], ss[b].t[:, :], AF.Sqrt), reads=sk, writes=sk)
                    P.op("dve", lambda e, b=b: e.reciprocal(ss[b].t[:, :], ss[b].t[:, :]), reads=sk, writes=sk)
                    for h in range(G_H):
                        vs = slice(h * G_DV, (h + 1) * G_DV)
                        P.op("dve", lambda e, b=b, h=h, vs=vs: e.scalar_tensor_tensor(
                            tmp[b].t[:, vs], o_t[b].t[:, vs], ss[b].t[:, h:h + 1], gbc.t[:, vs], ALU.mult, ALU.mult),
                            reads=[o_t[b].k(h), ss[b].k(h), gbc], writes=[tmp[b].k(h)])
                        P.tt("pool", a_t[b].k(h)[:, vs], tmp[b].k(h)[:, vs], r_t[b][:, vs], ALU.mult)
                    for f in range(8):
                        tp = g.psb[f % 2][:, 512 + (f % 4) * 128:512 + (f % 4 + 1) * 128]
                        P.transpose(tp, a_t[b].k(f // 2)[:, f * 128:(f + 1) * 128], g.ident[:, :])
                        P.copy("act" if f % 2 == 0 else "dve", aT[b].k(f)[:, f, :], tp)
                    P.dma_multi("pool", atv.k(c)[:, 0:8, tok], aT[b].t[:, :, :], reads=[aT[b].k(f) for f in range(8)])


def run_gla(P, l):
    phase_gla_proj(P)
    phase_gla_scan(P)
    G.mix_out = (8, G.inp["g_w_out"].sub(0))

M_H = 4
M_DQK = 256
M_DV = 512
M_IN = 2048


def mlstm_dram(P):
    g = G
    if hasattr(g, "mXMT"):
        return
    g.mXMT = P.dram("mXMT", [M_IN, TT], BF16)
    g.mXCT = P.dram("mXCT", [M_IN, TT], BF16)
    g.mQKT = P.dram("mQKT", [2048, TT], BF16)
    g.mKTOK = P.dram("mKTOK", [TT, 1024], BF16)
    g.mV = P.dram("mV", [TT, 2048], BF16)
    g.mGT = P.dram("mGT", [TT, 16], F32)
    g.mOG = P.dram("mOG", [TT, 2048], BF16)
    g.mHS = P.dram("mHS", [TT, 2048], F32)


def phase_mlstm_up(P, j):
    g = G
    with P.phase() as ph:
        hT = ph.sbuf("hT", [128, 8, TT], BF16)
        hv = g.HT.view("(k p) n -> p k n", p=128)
        for kc in range(8):
            P.dma("sp" if kc % 2 == 0 else "act", hT.k(kc)[:, kc, :], hv[:, kc, :])
        cw = ph.sbuf("cw", [128, 16, 3], F32)
        cb = ph.sbuf("cb", [128, 16], F32)
        P.dma("sp", cw[:, :, :], g.inp["m_cw_fm"][j, :, :, :])
        P.dma("sp", cb[:, :], g.inp["m_cb_fm"][j, :, :])
        GW = TT + 4
        Gb = [ph.sbuf(f"G{i}", [128, GW], F32) for i in range(2)]
        Tb = [ph.sbuf(f"Tm{i}", [128, TT], F32) for i in range(2)]
        Xm = [ph.sbuf(f"Xm{i}", [128, TT], BF16) for i in range(2)]
        Xc = [ph.sbuf(f"Xc{i}", [128, TT], BF16) for i in range(2)]
        wst = [ph.sbuf(f"wst{i}", [128, 8, 128], F32) for i in range(2)]
        wb = [ph.sbuf(f"wb{i}", [128, 8, 128], BF16) for i in range(2)]
        for i in range(2):
            P.op("pool", lambda e, i=i: e.memset(Gb[i].t[:, :], 0.0), reads=[], writes=[Gb[i].k(ti) for ti in range(9)])
        wup = g.inp["m_w_up"].view("l (k p) n -> l p k n", p=128)
        xmv = g.mXMT.view("(f p) n -> f p n", p=128)
        xcv = g.mXCT.view("(f p) n -> f p n", p=128)
        for f in range(16):
            i = f % 2
            P.dma("sp", wst[i][:, :, :], wup[j, :, :, f * 128:(f + 1) * 128])
            P.copy("pool", wb[i][:, :, :], wst[i][:, :, :])
            Gf, Tf = Gb[i], Tb[i]
            for ti, (s, n, cnd) in enumerate(TILES512):
                ps = g.ps[ti % 6]
                P.mm(ps[:, 0:n], [(wb[i][:, kc, :], hT.k(kc)[:, kc, s:s + n]) for kc in range(8)])
                o = (1 if cnd == 1 else 3) + s
                P.activation("act", Gf.k(ti)[:, o:o + n], ps[:, 0:n], AF.Copy)
            gr = [Gf.k(ti) for ti in range(9)]
            for (s, n, o) in ((0, TC, 1), (TC, T, 3)):
                P.op("pool", lambda e, o=o, s=s, n=n, i=i: e.tensor_copy(Xm[i].t[:, s:s + n], Gb[i].t[:, o + s:o + s + n]),
                     reads=gr, writes=[Xm[i]])
                P.op("dve", lambda e, o=o, s=s, n=n, f=f, i=i: e.tensor_scalar(
                    Tb[i].t[:, s:s + n], Gb[i].t[:, o + s - 1:o + s - 1 + n], cw.t[:, f, 0:1], None, ALU.mult),
                    reads=gr + [cw], writes=[Tf])
                for tap in (1, 2):
                    P.op("dve", lambda e, o=o, s=s, n=n, tap=tap, f=f, i=i: e.scalar_tensor_tensor(
                        Tb[i].t[:, s:s + n], Gb[i].t[:, o + s - 1 + tap:o + s - 1 + tap + n], cw.t[:, f, tap:tap + 1],
                        Tb[i].t[:, s:s + n], ALU.mult, ALU.add), reads=gr + [cw, Tf], writes=[Tf])
                P.op("act", lambda e, s=s, n=n, f=f, i=i: e.activation(Xc[i].t[:, s:s + n], Tb[i].t[:, s:s + n], AF.Silu,
                                                                   bias=cb.t[:, f:f + 1]), reads=[Tf, cb], writes=[Xc[i]])
            P.dma("pool", xmv.k(f)[f, :, :], Xm[i][:, :])
            P.dma("pool", xcv.k(f)[f, :, :], Xc[i][:, :])


def phase_mlstm_qk(P, j):
    g = G
    with P.phase() as ph:
        stage = [ph.sbuf(f"st{i}", [128, 2048], F32) for i in range(2)]
        W = ph.sbuf("Wqk", [128, 16, 2048], BF16)
        Wg = ph.sbuf("Wg", [128, 16, 16], BF16)
        bgb = ph.sbuf("bgb", [128, 16], F32)
        load_w(P, ph, W, g.inp["m_w_qk"].sub(j), 0, 2048, 0, 2048, stage)
        load_w(P, ph, Wg, g.inp["m_wg"].sub(j), 0, 2048, 0, 16, stage)
        P.dma("sp", bgb[:, :], g.inp["m_bg_bc"][j, :, :])
        NB = 2
        xc = [ph.sbuf(f"xc{i}", [128, 16, 512], BF16) for i in range(NB)]
        qk = [ph.sbuf(f"qk{i}", [128, 16, 512], BF16) for i in range(NB)]
        kt = [ph.sbuf(f"kt{i}", [128, 1024], BF16) for i in range(NB)]
        gt = [ph.sbuf(f"gt{i}", [128, 16], F32) for i in range(NB)]
        xcv = g.mXCT.view("(k p) n -> p k n", p=128)
        qkv = g.mQKT.view("(f p) n -> p f n", p=128)
        sub_i = 0
        for ti, (s, n, cnd) in enumerate(TILES512):
            b = ti % NB
            P.dma("sp" if ti % 2 == 0 else "act", xc[b][:, :, 0:n], xcv[:, :, s:s + n])
            for oc in range(16):
                ps = g.ps[oc % 4]
                P.mm(ps[:, 0:n], [(W.k(kc)[:, kc, oc * 128:(oc + 1) * 128], xc[b][:, kc, 0:n]) for kc in range(16)])
                if oc < 8:
                    P.copy("act" if oc % 2 == 0 else "dve", qk[b].k(oc)[:, oc, 0:n], ps[:, 0:n])
                else:
                    P.activation("act", qk[b].k(oc)[:, oc, 0:n], ps[:, 0:n], AF.Copy, scale=float(M_DQK ** -0.5))
            P.dma_multi("pool", qkv.k(ti)[:, :, s:s + n], qk[b].t[:, :, 0:n], reads=[qk[b].k(oc) for oc in range(16)])
            for sb in range(n // 128):
                bb = sub_i % NB
                sub_i += 1
                t0 = s + sb * 128
                for oc in range(8):
                    tp = g.psb[oc % 2][:, (oc // 2) * 128:(oc // 2 + 1) * 128]
                    P.transpose(tp, qk[b].k(8 + oc)[:, 8 + oc, sb * 128:(sb + 1) * 128], g.ident[:, :])
                    P.copy("dve" if oc % 2 == 0 else "pool" if False else "dve", kt[bb].k(oc)[:, oc * 128:(oc + 1) * 128], tp)
                P.dma_multi("pool", g.mKTOK.k(t0)[t0:t0 + 128, :], kt[bb].t[:, :], reads=[kt[bb].k(oc) for oc in range(8)])
                ps = g.ps[4 + sb % 2]
                P.mm(ps[:, 0:16], [(xc[b][:, kc, sb * 128:(sb + 1) * 128], Wg.k(kc)[:, kc, :]) for kc in range(16)])
                P.tt("dve", gt[bb][:, :], ps[:, 0:16], bgb[:, :], ALU.add)
                P.dma("pool", g.mGT.k(t0)[t0:t0 + 128, :], gt[bb][:, :])


def phase_mlstm_v(P, j):
    g = G
    with P.phase() as ph:
        stage = [ph.sbuf(f"st{i}", [128, 1024], F32) for i in range(2)]
        Wv = ph.sbuf("Wv", [128, 16, 2048], BF16)
        Wo = ph.sbuf("Wo", [128, 8, 2048], BF16)
        load_w(P, ph, Wv, g.inp["m_w_v"].sub(j), 0, 2048, 0, 2048, stage)
        load_w(P, ph, Wo, g.inp["m_w_og"].sub(j), 0, 1024, 0, 2048, stage)
        NB = 2
        xm = [ph.sbuf(f"xm{i}", [128, 16, 512], BF16) for i in range(NB)]
        hh = [ph.sbuf(f"hh{i}", [128, 8, 512], BF16) for i in range(NB)]
        vt = [ph.sbuf(f"vt{i}", [128, 2048], BF16) for i in range(NB)]
        ot = [ph.sbuf(f"ot{i}", [128, 2048], BF16) for i in range(NB)]
        xmv = g.mXMT.view("(k p) n -> p k n", p=128)
        hv = g.HT.view("(k p) n -> p k n", p=128)
        sub_i = 0
        for ti, (s, n, cnd) in enumerate(TILES512):
            b = ti % NB
            P.dma("sp", xm[b][:, :, 0:n], xmv[:, :, s:s + n])
            P.dma("act", hh[b][:, :, 0:n], hv[:, :, s:s + n])
            for sb in range(n // 128):
                bb = sub_i % NB
                sub_i += 1
                t0 = s + sb * 128
                ss_ = slice(sb * 128, (sb + 1) * 128)
                for q4 in range(4):
                    ps = g.ps[q4 % 3]
                    P.mm(ps[:, :], [(xm[b][:, kc, ss_], Wv.k(kc)[:, kc, q4 * 512:(q4 + 1) * 512]) for kc in range(16)])
                    P.copy("dve", vt[bb].k(q4)[:, q4 * 512:(q4 + 1) * 512], ps[:, :])
                P.dma_multi("pool", g.mV.k(t0)[t0:t0 + 128, :], vt[bb].t[:, :], reads=[vt[bb].k(q) for q in range(4)])
                for q4 in range(4):
                    ps = g.ps[3 + q4 % 3]
                    P.mm(ps[:, :], [(hh[b][:, kc, ss_], Wo.k(kc)[:, kc, q4 * 512:(q4 + 1) * 512]) for kc in range(8)])
                    P.activation("act", ot[bb].k(q4)[:, q4 * 512:(q4 + 1) * 512], ps[:, :], AF.Sigmoid)
                P.dma_multi("pool", g.mOG.k(t0)[t0:t0 + 128, :], ot[bb].t[:, :], reads=[ot[bb].k(q) for q in range(4)])


def phase_mlstm_scan(P, j):
    g = G
    with P.phase() as ph:
        triN = []
        for nm, src in (("tnl", g.tri_le), ("tng", g.tri_ge)):
            t = ph.sbuf(nm, [128, 128], F32)
            P.ts("dve", t[:, :], src[:, :], -1.0, ALU.mult)
            triN.append(t)
        negones = ph.sbuf("negones", [128, 128], F32)
        P.memset("pool", negones[:, :], -1.0)
        gbc = ph.sbuf("gbc", [128, 2048], F32)
        P.dma("sp", gbc[:, :], g.inp["m_ng_bc"][j, :, :])
        C = [ph.sbuf(f"C{h}", [128, 2, 512], F32) for h in range(M_H)]
        Cb = [ph.sbuf(f"Cb{h}", [128, 2, 512], BF16) for h in range(M_H)]
        nn = [ph.sbuf(f"n{h}", [128, 2, 2], F32) for h in range(M_H)]
        nb = [ph.sbuf(f"nb{h}", [128, 2, 2], BF16) for h in range(M_H)]
        NB = 2
        qT = [ph.sbuf(f"qT{i}", [128, 16, 128], BF16) for i in range(NB)]
        ktk = [ph.sbuf(f"ktk{i}", [128, 1024], BF16) for i in range(NB)]
        v_t = [ph.sbuf(f"v{i}", [128, 2048], BF16) for i in range(NB)]
        g_t = [ph.sbuf(f"g{i}", [128, 16], F32) for i in range(NB)]
        h0_t = [ph.sbuf(f"h0{i}", [128, 2048], F32) for i in range(NB)]
        og_t = [ph.sbuf(f"og{i}", [128, 2048], BF16) for i in range(NB)]
        H_t = [ph.sbuf(f"H{i}", [128, 2048], F32) for i in range(NB)]
        tmp = [ph.sbuf(f"tmp{i}", [128, 2048], F32) for i in range(NB)]
        a_t = [ph.sbuf(f"a{i}", [128, 2048], BF16) for i in range(NB)]
        aT = [ph.sbuf(f"aT{i}", [128, 16, 128], BF16) for i in range(NB)]
        ss = [ph.sbuf(f"ss{i}", [128, 4], F32) for i in range(NB)]
        junk = ph.sbuf("junk", [128, 512], F32)
        lsp = [ph.sbuf(f"lsp{i}", [128, 4], F32) for i in range(NB)]
        eb = [ph.sbuf(f"eb{i}", [128, 4], F32) for i in range(NB)]
        av = [ph.sbuf(f"av{i}", [128, 4], F32) for i in range(NB)]
        ebL = [ph.sbuf(f"ebL{i}", [128, 4], F32) for i in range(NB)]
        kw = [ph.sbuf(f"kw{i}", [128, 4], F32) for i in range(NB)]
        HB = 2
        attT = [ph.sbuf(f"att{i}", [128, 128], BF16) for i in range(HB)]
        kws = [ph.sbuf(f"kws{i}", [128, 256], BF16) for i in range(HB)]
        rr = [ph.sbuf(f"rr{i}", [128, 2], F32) for i in range(HB)]
        qkv = g.mQKT.view("(f p) n -> p f n", p=128)
        atv = g.AT.view("(f p) n -> p f n", p=128)
        it = 0
        for d in range(2):
            for h in range(M_H):
                P.memset("pool", C[h][:, :, :], 0.0)
                P.memset("pool", Cb[h][:, :, :], 0.0)
                P.memset("pool", nn[h][:, :, :], 0.0)
                P.memset("pool", nb[h][:, :, :], 0.0)
            mask = g.tri_le if d == 0 else g.tri_ge
            for ci, c in enumerate(scan_order(d)):
                b = ci % NB
                tok = slice(c * CH, (c + 1) * CH)
                P.dma("sp", qT[b][:, :, :], qkv[:, :, tok])
                P.dma("act", ktk[b][:, :], g.mKTOK[tok, :])
                P.dma("sp", v_t[b][:, :], g.mV[tok, :])
                P.dma("act", g_t[b][:, :], g.mGT[tok, :])
                if d == 1:
                    P.dma("sp", h0_t[b][:, :], g.mHS.k(c)[tok, :])
                    P.dma("act", og_t[b][:, :], g.mOG[tok, :])
                gi = g_t[b][:, d * 8:d * 8 + 4]
                gf = g_t[b][:, d * 8 + 4:d * 8 + 8]
                P.activation("act", lsp[b][:, :], gf, AF.Exp, scale=-1.0)
                P.activation("act", lsp[b][:, :], lsp[b][:, :], AF.Ln, bias=1.0)
                gps = g.ps[0]
                P.mm(gps[:, 0:4], [(triN[d][:, :], lsp[b][:, :])])
                P.mm(gps[:, 4:8], [(negones[:, :], lsp[b][:, :])])
                P.activation("act", eb[b][:, :], gps[:, 0:4], AF.Exp)
                P.activation("act", ebL[b][:, :], gps[:, 4:8], AF.Exp)
                P.tt("dve", av[b][:, :], gi, gps[:, 0:4], ALU.subtract)
                P.activation("act", av[b][:, :], av[b][:, :], AF.Exp)
                P.tt("dve", kw[b][:, :], av[b][:, :], ebL[b][:, :], ALU.mult)
                for h in range(M_H):
                    hb = it % HB
                    it += 1
                    vs = slice(h * M_DV, (h + 1) * M_DV)
                    sps = g.ps[1 + hb]
                    P.mm(sps[:, 0:128], [(qT[b][:, 8 + 2 * h + dd, :], qT[b][:, 2 * h + dd, :]) for dd in range(2)])
                    P.stt(attT[hb][:, :], sps[:, 0:128], av[b][:, h:h + 1], mask[:, :], ALU.mult, ALU.mult)
                    ops = g.ps[3 + hb]
                    P.mm(ops[:, :], [(attT[hb][:, :], v_t[b][:, vs]),
                                     (qT[b][:, 2 * h, :], Cb[h][:, 0, :]),
                                     (qT[b][:, 2 * h + 1, :], Cb[h][:, 1, :])])
                    P.mm(sps[:, 128:130], [(attT[hb][:, :], g.ones[:, 0:2]),
                                           (qT[b][:, 2 * h, :], nb[h][:, 0, :]),
                                           (qT[b][:, 2 * h + 1, :], nb[h][:, 1, :])])
                    P.ts("dve", rr[hb][:, 0:1], sps[:, 128:129], eb[b][:, h:h + 1], ALU.mult)
                    P.ts("dve", rr[hb][:, 1:2], rr[hb][:, 0:1], -1.0, ALU.mult, s2=1.0, op1=ALU.max)
                    P.tt("dve", rr[hb][:, 0:1], rr[hb][:, 0:1], rr[hb][:, 1:2], ALU.max)
                    P.recip(rr[hb][:, 0:1], rr[hb][:, 0:1])
                    P.tt("dve", rr[hb][:, 1:2], rr[hb][:, 0:1], eb[b][:, h:h + 1], ALU.mult)
                    if d == 0:
                        P.activation("act", H_t[b].k(h)[:, vs], ops[:, :], AF.Identity, scale=rr[hb][:, 1:2])
                    else:
                        P.stt(H_t[b].k(h)[:, vs], ops[:, :], rr[hb][:, 1:2], h0_t[b][:, vs], ALU.mult, ALU.add)
                    P.ts("pool", kws[hb][:, :], ktk[b][:, h * 256:(h + 1) * 256], kw[b][:, h:h + 1], ALU.mult)
                    for dd in range(2):
                        dps = g.ps[5] if dd == 0 else g.ps[0]
                        P.mm(dps[:, :], [(kws[hb][:, dd * 128:(dd + 1) * 128], v_t[b][:, vs])])
                        P.stt(C[h].k(dd)[:, dd, :], C[h].k(dd)[:, dd, :], ebL[b][:, h:h + 1], dps[:, :], ALU.mult, ALU.add)
                        P.copy("act", Cb[h].k(dd)[:, dd, :], C[h].k(dd)[:, dd, :])
                    P.mm_multi([(sps[:, 132 + 2 * dd:134 + 2 * dd], [(kws[hb][:, dd * 128:(dd + 1) * 128], g.ones[:, 0:2])])
                                for dd in range(2)])
                    P.stt(nn[h][:, :, :], nn[h][:, :, :], ebL[b][:, h:h + 1],
                          sps.view("p (a c) -> p a c", c=2)[:, 66:68, :], ALU.mult, ALU.add)
                    P.copy("pool", nb[h][:, :, :], nn[h][:, :, :])
                hk = [H_t[b].k(h) for h in range(M_H)]
                if d == 0:
                    P.dma_multi("pool", g.mHS.k(c)[tok, :], H_t[b].t[:, :], reads=hk)
                else:
                    for h in range(M_H):
                        vs = slice(h * M_DV, (h + 1) * M_DV)
                        P.tt("dve", tmp[b].k(h)[:, vs], H_t[b].k(h)[:, vs], og_t[b][:, vs], ALU.mult)
                        P.op("act", lambda e, b=b, h=h, vs=vs: e.activation(junk.t[:, :], tmp[b].t[:, vs], AF.Square,
                                                                          accum_out=ss[b].t[:, h:h + 1]),
                             reads=[tmp[b].k(h)], writes=[junk, ss[b].k(h)])
                    sk = [ss[b].k(h) for h in range(M_H)]
                    P.op("dve", lambda e, b=b: e.tensor_scalar(ss[b].t[:, :], ss[b].t[:, :], 1.0 / M_DV, EPS, ALU.mult, ALU.add),
                         reads=sk, writes=sk)
                    P.op("act", lambda e, b=b: e.activation(ss[b].t[:, :], ss[b].t[:, :], AF.Sqrt), reads=sk, writes=sk)
                    P.op("dve", lambda e, b=b: e.reciprocal(ss[b].t[:, :], ss[b].t[:, :]), reads=sk, writes=sk)
                    for h in range(M_H):
                        vs = slice(h * M_DV, (h + 1) * M_DV)
                        P.op("dve", lambda e, b=b, h=h, vs=vs: e.scalar_tensor_tensor(
                            a_t[b].t[:, vs], tmp[b].t[:, vs], ss[b].t[:, h:h + 1], gbc.t[:, vs], ALU.mult, ALU.mult),
                            reads=[tmp[b].k(h), ss[b].k(h), gbc], writes=[a_t[b].k(h)])
                    for f in range(16):
                        tp = g.psb[f % 2][:, 256 + (f % 4) * 128:256 + (f % 4 + 1) * 128]
                        P.transpose(tp, a_t[b].k(f // 4)[:, f * 128:(f + 1) * 128], g.ident[:, :])
                        P.copy("act" if f % 2 == 0 else "pool" if False else "act", aT[b].k(f)[:, f, :], tp)
                    P.dma_multi("pool", atv.k(c)[:, 0:16, tok], aT[b].t[:, :, :], reads=[aT[b].k(f) for f in range(16)])


def run_mlstm(P, l):
    j = l // 3
    mlstm_dram(P)
    phase_mlstm_up(P, j)
    phase_mlstm_qk(P, j)
    phase_mlstm_v(P, j)
    phase_mlstm_scan(P, j)
    G.mix_out = (16, G.inp["m_w_out"].sub(j))

A_H = 8


def mla_dram(P):
    g = G
    g.aQNT = P.dram("aQNT", [384, TT], BF16)
    g.aCKVT = P.dram("aCKVT", [256, TT], BF16)
    g.aKRT = P.dram("aKRT", [64, TT], BF16)
    g.aQT = P.dram("aQT", [A_H, 128, TT], BF16)
    g.aQR = P.dram("aQR", [A_H, 64, TT], BF16)
    g.aKT = P.dram("aKT", [A_H, 128, TT], BF16)
    g.aV = P.dram("aV", [TT, 1024], BF16)


def phase_mla_down(P):
    g = G
    with P.phase() as ph:
        hT = ph.sbuf("hT", [128, 8, TT], BF16)
        hv = g.HT.view("(k p) n -> p k n", p=128)
        for kc in range(8):
            P.dma("sp" if kc % 2 == 0 else "act", hT.k(kc)[:, kc, :], hv[:, kc, :])
        stage = [ph.sbuf(f"st{i}", [128, 384], F32) for i in range(2)]
        Wdq = ph.sbuf("Wdq", [128, 8, 384], BF16)
        Wdkv = ph.sbuf("Wdkv", [128, 8, 320], BF16)
        Wsw = ph.sbuf("Wsw", [128, 8, 64], BF16)
        load_w(P, ph, Wdq, g.inp["a_w_dq"].sub(0), 0, 1024, 0, 384, stage)
        load_w(P, ph, Wdkv, g.inp["a_w_dkv"].sub(0), 0, 1024, 0, 320, stage)
        load_w(P, ph, Wsw, g.inp["a_dkv_sw"], 0, 1024, 0, 64, stage)
        qnb = ph.sbuf("qnb", [128, 384], F32)
        kvnb = ph.sbuf("kvnb", [128, 256], F32)
        P.dma("sp", qnb[:, :], g.inp["a_qn_bc"][0, :, :])
        P.dma("sp", kvnb[:, :], g.inp["a_kvn_bc"][0, :, :])
        cos = ph.sbuf("cos", [64, TT], F32)
        sin = ph.sbuf("sin", [64, TT], F32)
        P.dma("sp", cos[:, :], g.inp["rope_cos"][:, :])
        P.dma("act", sin[:, :], g.inp["rope_sin"][:, :])
        NB = 2
        qn = [ph.sbuf(f"qn{i}", [128, 384], BF16) for i in range(NB)]
        kn = [ph.sbuf(f"kn{i}", [128, 256], BF16) for i in range(NB)]
        qnT = [ph.sbuf(f"qnT{i}", [128, 3, 128], BF16) for i in range(NB)]
        knT = [ph.sbuf(f"knT{i}", [128, 2, 128], BF16) for i in range(NB)]
        ss = [ph.sbuf(f"ss{i}", [128, 2], F32) for i in range(NB)]
        junk = ph.sbuf("junk", [128, 384], F32)
        qntv = g.aQNT.view("(f p) n -> p f n", p=128)
        ckvv = g.aCKVT.view("(f p) n -> p f n", p=128)
        for c in range(NCH):
            b = c % NB
            tok = slice(c * CH, (c + 1) * CH)
            lhs = [hT.k(kc)[:, kc, tok] for kc in range(8)]
            for (ps, W, w0, wn, dst, nbc, col, dT, dview, nf) in (
                    (g.ps[0], Wdq, 0, 384, qn[b], qnb, 0, qnT[b], qntv, 3),
                    (g.ps[1], Wdkv, 0, 256, kn[b], kvnb, 1, knT[b], ckvv, 2)):
                P.mm(ps[:, 0:wn], [(lhs[kc], W.k(kc)[:, kc, w0:w0 + wn]) for kc in range(8)])
                P.op("act", lambda e, ps=ps, wn=wn, b=b, col=col: e.activation(
                    junk.t[:, 0:wn], ps.t[:, 0:wn], AF.Square, accum_out=ss[b].t[:, col:col + 1]),
                    reads=[ps], writes=[junk, ss[b].k(col)])
                P.op("dve", lambda e, b=b, col=col, wn=wn: e.tensor_scalar(
                    ss[b].t[:, col:col + 1], ss[b].t[:, col:col + 1], 1.0 / wn, EPS, ALU.mult, ALU.add),
                    reads=[ss[b].k(col)], writes=[ss[b].k(col)])
                P.op("act", lambda e, b=b, col=col: e.activation(ss[b].t[:, col:col + 1], ss[b].t[:, col:col + 1], AF.Sqrt),
                     reads=[ss[b].k(col)], writes=[ss[b].k(col)])
                P.op("dve", lambda e, b=b, col=col: e.reciprocal(ss[b].t[:, col:col + 1], ss[b].t[:, col:col + 1]),
                     reads=[ss[b].k(col)], writes=[ss[b].k(col)])
                P.op("dve", lambda e, ps=ps, wn=wn, b=b, col=col, dst=dst, nbc=nbc: e.scalar_tensor_tensor(
                    dst.t[:, 0:wn], ps.t[:, 0:wn], ss[b].t[:, col:col + 1], nbc.t[:, 0:wn], ALU.mult, ALU.mult),
                    reads=[ps, ss[b].k(col), nbc], writes=[dst])
                for f in range(nf):
                    tp = g.psb[col][:, f * 128:(f + 1) * 128]
                    P.transpose(tp, dst[:, f * 128:(f + 1) * 128], g.ident[:, :])
                    P.copy("act" if f % 2 == 0 else "dve", dT.k(f)[:, f, :], tp)
                P.dma_multi("pool", dview.k(c)[:, 0:nf, tok], dT.t[:, :, :], reads=[dT.k(f) for f in range(nf)])
        kr = [ph.sbuf(f"kr{i}", [64, 512], BF16) for i in range(NB)]
        t1 = [ph.sbuf(f"t1{i}", [64, 512], F32) for i in range(NB)]
        t2 = [ph.sbuf(f"t2{i}", [64, 512], F32) for i in range(NB)]
        for ti, (s, n, cnd) in enumerate(TILES512):
            b = ti % NB
            pa, pb = g.ps[2 + (ti % 2) * 2], g.ps[3 + (ti % 2) * 2]
            P.mm(pa[0:64, 0:n], [(Wdkv.k(kc)[:, kc, 256:320], hT.k(kc)[:, kc, s:s + n]) for kc in range(8)])
            P.mm(pb[0:64, 0:n], [(Wsw.k(kc)[:, kc, :], hT.k(kc)[:, kc, s:s + n]) for kc in range(8)])
            P.tt("dve", t1[b][:, 0:n], pa[0:64, 0:n], cos[:, s:s + n], ALU.mult)
            P.tt("dve", t2[b][:, 0:n], pb[0:64, 0:n], sin[:, s:s + n], ALU.mult)
            P.tt("pool", kr[b][:, 0:n], t1[b][:, 0:n], t2[b][:, 0:n], ALU.add)
            P.dma("pool", g.aKRT.k(ti)[:, s:s + n], kr[b][:, 0:n])


def phase_mla_up(P):
    g = G
    with P.phase() as ph:
        qnT = ph.sbuf("qnT", [128, 3, TT], BF16)
        knT = ph.sbuf("knT", [128, 2, TT], BF16)
        qv = g.aQNT.view("(f p) n -> p f n", p=128)
        kv = g.aCKVT.view("(f p) n -> p f n", p=128)
        for f in range(3):
            P.dma("sp", qnT.k(f)[:, f, :], qv[:, f, :])
        for f in range(2):
            P.dma("act", knT.k(f)[:, f, :], kv[:, f, :])
        stage = [ph.sbuf(f"st{i}", [128, 2048], F32) for i in range(2)]
        Wuq = ph.sbuf("Wuq", [128, 3, 1536], BF16)
        Wsw = ph.sbuf("Wsw", [128, 3, 512], BF16)
        Wukv = ph.sbuf("Wukv", [128, 2, 2048], BF16)
        load_w(P, ph, Wuq, g.inp["a_w_uq"].sub(0), 0, 384, 0, 1536, stage)
        load_w(P, ph, Wsw, g.inp["a_uq_sw"], 0, 384, 0, 512, stage)
        load_w(P, ph, Wukv, g.inp["a_w_ukv"].sub(0), 0, 256, 0, 2048, stage)
        cos = ph.sbuf("cos", [64, TT], F32)
        sin = ph.sbuf("sin", [64, TT], F32)
        P.dma("sp", cos[:, :], g.inp["rope_cos"][:, :])
        P.dma("act", sin[:, :], g.inp["rope_sin"][:, :])
        NB = 3
        qo = [ph.sbuf(f"qo{i}", [128, 512], BF16) for i in range(NB)]
        ko = [ph.sbuf(f"ko{i}", [128, 512], BF16) for i in range(NB)]
        qr = [ph.sbuf(f"qr{i}", [64, 512], BF16) for i in range(NB)]
        t1 = [ph.sbuf(f"t1{i}", [64, 512], F32) for i in range(NB)]
        t2 = [ph.sbuf(f"t2{i}", [64, 512], F32) for i in range(NB)]
        it = 0
        for h in range(A_H):
            for ti, (s, n, cnd) in enumerate(TILES512):
                b = it % NB
                it += 1
                ts_ = slice(s, s + n)
                p0 = g.ps[0]
                P.mm(p0[:, 0:n], [(Wuq.k(kc)[:, kc, h * 192:h * 192 + 128], qnT.k(kc)[:, kc, ts_]) for kc in range(3)])
                P.copy("act", qo[b][:, 0:n], p0[:, 0:n])
                P.dma("pool", g.aQT.k((h, ti))[h, :, ts_], qo[b][:, 0:n])
                p1 = g.ps[1]
                P.mm(p1[:, 0:n], [(Wukv.k(kc)[:, kc, h * 256:h * 256 + 128], knT.k(kc)[:, kc, ts_]) for kc in range(2)])
                P.copy("act", ko[b][:, 0:n], p1[:, 0:n])
                P.dma("pool", g.aKT.k((h, ti))[h, :, ts_], ko[b][:, 0:n])
                pa, pb = g.ps[2 + (it % 2) * 2], g.ps[3 + (it % 2) * 2]
                P.mm(pa[0:64, 0:n], [(Wuq.k(kc)[:, kc, h * 192 + 128:h * 192 + 192], qnT.k(kc)[:, kc, ts_]) for kc in range(3)])
                P.mm(pb[0:64, 0:n], [(Wsw.k(kc)[:, kc, h * 64:(h + 1) * 64], qnT.k(kc)[:, kc, ts_]) for kc in range(3)])
                P.tt("dve", t1[b][:, 0:n], pa[0:64, 0:n], cos[:, ts_], ALU.mult)
                P.tt("dve", t2[b][:, 0:n], pb[0:64, 0:n], sin[:, ts_], ALU.mult)
                P.tt("pool", qr[b][:, 0:n], t1[b][:, 0:n], t2[b][:, 0:n], ALU.add)
                P.dma("pool", g.aQR.k((h, ti))[h, :, ts_], qr[b][:, 0:n])
        vt = [ph.sbuf(f"vt{i}", [128, 1024], BF16) for i in range(2)]
        wv = Wukv.view("p k (h c) -> p k h c", c=256)
        for c in range(NCH):
            b = c % 2
            tok = slice(c * CH, (c + 1) * CH)
            for half in range(2):
                ps = g.ps[half]
                P.mm(ps[:, :], [(knT.k(kc)[:, kc, tok], View(Wukv, wv.t[:, kc, half * 4:(half + 1) * 4, 128:256]))
                                for kc in range(2)], extra_reads=[Wukv.k(0), Wukv.k(1)])
                P.copy("dve", vt[b].k(half)[:, half * 512:(half + 1) * 512], ps[:, :])
            P.dma_multi("pool", g.aV.k(c)[tok, :], vt[b].t[:, :], reads=[vt[b].k(0), vt[b].k(1)])


def phase_mla_attn(P):
    g = G
    with P.phase() as ph:
        KR = ph.sbuf("KR", [128, TT], BF16)
        P.memset("pool", KR[:, :], 0.0)
        P.dma("sp", KR[0:64, :], g.aKRT[:, :])
        NB = 2
        Kh = [ph.sbuf(f"Kh{i}", [128, TT], BF16) for i in range(NB)]
        Qh = [ph.sbuf(f"Qh{i}", [128, TT], BF16) for i in range(NB)]
        QRh = [ph.sbuf(f"QRh{i}", [128, TT], BF16) for i in range(NB)]
        Vh = [ph.sbuf(f"Vh{i}", [128, NCH, 128], BF16) for i in range(NB)]
        for i in range(NB):
            P.memset("pool", QRh[i][:, :], 0.0)
        NP = 4
        pt = [ph.sbuf(f"pt{i}", [128, 512], BF16) for i in range(NP)]
        rd = [ph.sbuf(f"rd{i}", [128, 512], F32) for i in range(2)]
        yo = [ph.sbuf(f"yo{i}", [128, 512], BF16) for i in range(2)]
        vv = g.aV.view("(t p) c -> p t c", p=128)
        scale = float(192 ** -0.5)
        it = 0
        qi = 0
        for h in range(A_H):
            hb = h % NB
            P.dma("sp", Kh[hb][:, :], g.aKT[h, :, :])
            P.dma("act", Qh[hb][:, :], g.aQT[h, :, :])
            P.dma("sp", QRh[hb][0:64, :], g.aQR[h, :, :])
            for q4 in range(4):
                t0_, t1_ = (q4 * NCH) // 4, ((q4 + 1) * NCH) // 4
                P.dma("act", Vh[hb][:, t0_:t1_, :], vv[:, t0_:t1_, h * 128:(h + 1) * 128])
            for ti, (s, n, cnd) in enumerate(TILES512):
                nkt = TC // CH if cnd == 1 else NCH
                po = g.ps[2 + qi % 2]
                pd = g.ps[4 + qi % 2]

                def smm(kt):
                    ks = slice(kt * CH, (kt + 1) * CH)
                    psx = g.ps[(it + kt) % 2]
                    P.mm(psx[:, 0:n], [(Kh[hb][:, ks], Qh[hb][:, s:s + n]), (KR[:, ks], QRh[hb][:, s:s + n])])

                smm(0)
                for kt in range(nkt):
                    psx = g.ps[(it + kt) % 2]
                    pb = pt[(it + kt) % NP]
                    P.activation("act", pb[:, 0:n], psx[:, 0:n], AF.Exp, scale=scale)
                    if kt + 1 < nkt:
                        smm(kt + 1)
                    P.mm1(po[:, 0:n], Vh[hb][:, kt, :], pb[:, 0:n], kt == 0, kt == nkt - 1)
                    P.mm1(pd[:, 0:n], g.ones[:, :], pb[:, 0:n], kt == 0, kt == nkt - 1)
                it += nkt
                r = rd[qi % 2]
                y = yo[qi % 2]
                P.recip(r[:, 0:n], pd[:, 0:n])
                P.tt("dve", y[:, 0:n], po[:, 0:n], r[:, 0:n], ALU.mult)
                P.dma("pool", g.AT.k((h, ti))[h * 128:(h + 1) * 128, s:s + n], y[:, 0:n])
                qi += 1


def run_mla(P, l):
    import os
    stop = int(os.environ.get("MLA_STOP", "9"))
    mla_dram(P)
    phase_mla_down(P)
    if stop >= 2:
        phase_mla_up(P)
    if stop >= 3:
        phase_mla_attn(P)
    G.mix_out = (8, G.inp["a_w_out"].sub(0))

def _fm(v, k):
    return np.ascontiguousarray(np.asarray(v, np.float32).reshape(k, 128).T)


def rope_tables():
    rows = T // 64
    row = np.repeat(np.arange(rows, dtype=np.float32), 64)
    col = np.tile(np.arange(64, dtype=np.float32), rows)
    n_freq = 16
    inv_freq = (np.float32(10000.0) ** (-np.arange(n_freq, dtype=np.float32) / n_freq)).astype(np.float32)
    ar = row[:, None] * inv_freq
    ac = col[:, None] * inv_freq
    cos = np.ones((64, TT), np.float32)
    sin = np.zeros((64, TT), np.float32)
    cr, sr, cc, sc = np.cos(ar).T, np.sin(ar).T, np.cos(ac).T, np.sin(ac).T
    cos[0:16, TC:] = cr
    cos[16:32, TC:] = cr
    cos[32:48, TC:] = cc
    cos[48:64, TC:] = cc
    sin[0:16, TC:] = -sr
    sin[16:32, TC:] = sr
    sin[32:48, TC:] = -sc
    sin[48:64, TC:] = sc
    return cos, sin


ROPE_PERM = np.concatenate([np.arange(16, 32), np.arange(0, 16), np.arange(48, 64), np.arange(32, 48)])


def prep_shared(inp):
    f32 = lambda a: np.ascontiguousarray(np.asarray(a, np.float32))
    sh = {}
    for k in ("ada_w", "ffn_w_in", "ffn_w_out", "m_w_up", "m_w_qk", "m_w_v", "m_w_og", "m_w_out",
              "a_w_dq", "a_w_uq", "a_w_dkv", "a_w_ukv", "a_w_out", "g_w_qk", "g_w_v", "g_w_r", "g_w_out"):
        sh[k] = f32(inp[k])
    sh["ada_b_fm"] = np.concatenate([_fm(inp["ada_b"][l], 48) for l in range(DEPTH)], axis=1)
    ln = []
    for l in range(DEPTH):
        for w in range(2):
            ln.append(_fm(inp["ln_g"][l, w], 8))
            ln.append(_fm(inp["ln_b"][l, w], 8))
    sh["ln_fm"] = np.ascontiguousarray(np.concatenate(ln, axis=1))
    cw = np.asarray(inp["ffn_conv_w"], np.float32)
    sh["ffn_cw_fm"] = np.ascontiguousarray(cw.transpose(0, 2, 1).reshape(DEPTH, FK_FFN, 128, 3).transpose(0, 2, 1, 3))
    sh["ffn_cb_fm"] = np.ascontiguousarray(np.stack([_fm(inp["ffn_conv_b"][l], FK_FFN) for l in range(DEPTH)]))
    mcw = np.asarray(inp["m_conv_w"], np.float32)
    sh["m_cw_fm"] = np.ascontiguousarray(mcw.transpose(0, 2, 1).reshape(2, 16, 128, 3).transpose(0, 2, 1, 3))
    sh["m_cb_fm"] = np.ascontiguousarray(np.stack([_fm(inp["m_conv_b"][j], 16) for j in range(2)]))
    wg = np.asarray(inp["m_w_gates"], np.float32)
    sh["m_wg"] = np.ascontiguousarray(wg.transpose(0, 2, 1, 3).reshape(2, 2048, 16))
    bg = np.asarray(inp["m_b_gates"], np.float32).reshape(2, 16)
    sh["m_bg_bc"] = np.ascontiguousarray(np.broadcast_to(bg[:, None, :], (2, 128, 16)))
    sh["m_ng_bc"] = np.ascontiguousarray(np.broadcast_to(np.asarray(inp["m_norm_g"], np.float32)[:, None, :], (2, 128, 2048)))
    sh["g_ng_bc"] = np.ascontiguousarray(np.broadcast_to(np.asarray(inp["g_norm_g"], np.float32)[:, None, :], (1, 128, 1024)))
    sh["a_qn_bc"] = np.ascontiguousarray(np.broadcast_to(np.asarray(inp["a_q_norm"], np.float32)[:, None, :], (1, 128, 384)))
    sh["a_kvn_bc"] = np.ascontiguousarray(np.broadcast_to(np.asarray(inp["a_kv_norm"], np.float32)[:, None, :], (1, 128, 256)))
    a1 = np.asarray(inp["g_w_a1"], np.float32)[0]
    sh["g_wa1"] = np.ascontiguousarray(a1.transpose(1, 0, 2).reshape(1024, 32))
    a2 = np.asarray(inp["g_w_a2"], np.float32)[0]
    ba = np.asarray(inp["g_b_a"], np.float32)[0]
    sh["g_wa2b"] = np.ascontiguousarray(np.concatenate([a2, ba[:, None, :]], axis=1))
    uq = np.asarray(inp["a_w_uq"], np.float32)[0].reshape(384, 8, 192)
    sh["a_uq_sw"] = np.ascontiguousarray(uq[:, :, 128:][:, :, ROPE_PERM].reshape(384, 512))
    dkv = np.asarray(inp["a_w_dkv"], np.float32)[0]
    sh["a_dkv_sw"] = np.ascontiguousarray(dkv[:, 256:][:, ROPE_PERM])
    cos, sin = rope_tables()
    sh["rope_cos"] = cos
    sh["rope_sin"] = sin
    return sh


def prep_core(inp, b):
    x = np.asarray(inp["x"][b], np.float32)
    ctx = np.asarray(inp["ctx"][b], np.float32)
    xin = np.ascontiguousarray(np.concatenate([ctx.T, x.T], axis=1))
    cv = np.stack([_fm(inp["c"][b], 8), _fm(inp["c_ctx"], 8)], axis=2)
    return {"xin": xin, "cv": np.ascontiguousarray(cv)}


def declare_inputs(P, sh, core0):
    G.inp = {}
    for k, v in list(sh.items()) + list(core0.items()):
        G.inp[k] = P.dram(k, list(v.shape), F32, kind="ExternalInput")


def run_mixer(P, l):
    kind = l % 3
    if kind == 0:
        run_mlstm(P, l)
    elif kind == 1:
        run_mla(P, l)
    else:
        run_gla(P, l)


def build(sh, core0, mode="full"):
    nc = bass.Bass("TRN2", target_bir_lowering=False)
    P = Prog(nc)
    G.P = P
    declare_inputs(P, sh, core0)
    dbg = mode != "full"
    kind = "ExternalOutput" if dbg else "Internal"
    G.XR = P.dram("XR", [D, TT], F32, kind=kind)
    G.HT = P.dram("HT", [D, TT], BF16, kind=kind)
    G.AT = P.dram("AT", [FFN, TT], BF16, kind=kind)
    G.OUT = P.dram("OUT", [D, T], F32, kind="ExternalOutput")
    setup_globals(P, nc)
    for k in range(8):
        P.dma("sp" if k % 2 == 0 else "act", G.XR.k(k)[k * 128:(k + 1) * 128, :], G.inp["xin"][k * 128:(k + 1) * 128, :])
    P.barrier()
    phase_mod(P)
    if mode == "ffn_test":
        phase_modulate_first(P, 0, 3, 4)
        phase_ffn1(P, 0)
        phase_out_ln(P, 0, 1, FK_FFN, G.inp["ffn_w_out"].sub(0), 5, (1, 0, 1))
    elif mode.startswith("mix"):
        l = int(mode[3:])
        phase_modulate_first(P, l, 0, 1)
        run_mixer(P, l)
        FK, wsrc = G.mix_out
        phase_out_ln(P, l, 0, FK, wsrc, 2, (l, 3, 4))
    else:
        phase_modulate_first(P, 0, 0, 1)
        for l in range(DEPTH):
            last = l == DEPTH - 1
            run_mixer(P, l)
            FK, wsrc = G.mix_out
            phase_out_ln(P, l, 0, FK, wsrc, 2, (l, 3, 4), skip_ctx=last)
            phase_ffn1(P, l)
            phase_out_ln(P, l, 1, FK_FFN, G.inp["ffn_w_out"].sub(l), 5, None if last else (l + 1, 0, 1),
                         final=last, skip_ctx=last)
    P.emit()
    return nc, P


_CACHE = {}


def kernel(**inputs):
    sh = prep_shared(inputs)
    cores = [prep_core(inputs, b) for b in range(8)]
    nc, P = build(sh, cores[0], "full")
    in_maps = [{**sh, **cores[b]} for b in range(8)]
    res = run_bass_kernel_spmd(nc, in_maps, core_ids=list(range(8)))
    out = np.stack([np.ascontiguousarray(res.results[b]["OUT"].T) for b in range(8)], axis=0)
    return out.astype(np.float32)
```
